# Optimizing a Trainium2 kernel written in Bass

```python
import math
import jax, jax.numpy as jnp
from jax import lax
import numpy as np

D_MODEL = 1024
BATCH = 8
SEQ = 4096
DEPTH = 4

GRID_W = 64
D_MIX = D_MODEL
N_ATT_HEADS = 8
HEAD_DIM = 64
D_ATT = N_ATT_HEADS * HEAD_DIM
D_LRU = D_MIX - D_ATT
N_LRU_BLOCKS = 8
LRU_BLOCK = D_LRU // N_LRU_BLOCKS
WIN_H_MAX = 8
WIN_W = 16
CONV_LRU = 4
LRU_PAD = (2, 1)
LRU_C = 8.0
D_FF = 3 * D_MODEL
CONV_FFN = 3
D_IN = 3 * D_ATT + 2 * D_LRU
EPS = 1e-6

kernel_name = "hymba_natten_rglru_convffn_encoder"


def rmsnorm(x, g):
    xf = x.astype(jnp.float32)
    y = xf * lax.rsqrt(jnp.mean(xf * xf, axis=-1, keepdims=True) + EPS)
    return (y * g.astype(jnp.float32)).astype(x.dtype)


def depthwise_conv(x, w, b, pad):
    c = x.shape[-1]
    y = lax.conv_general_dilated(
        x, w[:, None, :].astype(x.dtype), window_strides=(1,), padding=[pad],
        dimension_numbers=("NWC", "WIO", "NWC"), feature_group_count=c)
    return y + b.astype(x.dtype)


def neighbourhood_attention(q, k, v, rel_bias):
    bsz, s, h, dh = q.shape
    rows = s // GRID_W
    kh = min(WIN_H_MAX, rows)
    qg = q.reshape(bsz, rows, GRID_W, h, dh)
    kg = k.reshape(bsz, rows, GRID_W, h, dh)
    vg = v.reshape(bsz, rows, GRID_W, h, dh)
    col = jnp.arange(GRID_W)
    col_start = jnp.clip(col - WIN_W // 2, 0, GRID_W - WIN_W)
    key_cols = col_start[:, None] + jnp.arange(WIN_W)[None, :]
    dc = key_cols - col[:, None] + (WIN_W - 1)
    scale = dh ** -0.5

    def row_block(args):
        r, q_row = args
        row_start = jnp.clip(r - kh // 2, 0, rows - kh)
        k_band = lax.dynamic_slice_in_dim(kg, row_start, kh, axis=1)
        v_band = lax.dynamic_slice_in_dim(vg, row_start, kh, axis=1)
        k_win = k_band[:, :, key_cols]
        v_win = v_band[:, :, key_cols]
        sc = jnp.einsum("bqhd,biqjhd->bhqij", q_row, k_win,
                        preferred_element_type=jnp.float32) * scale
        dr = row_start + jnp.arange(kh) - r + (WIN_H_MAX - 1)
        bias = rel_bias[:, dr[None, :, None], dc[:, None, :]]
        sc = sc + bias.astype(jnp.float32)[None]
        p = jax.nn.softmax(sc.reshape(bsz, h, GRID_W, kh * WIN_W), axis=-1)
        p = p.reshape(bsz, h, GRID_W, kh, WIN_W).astype(v.dtype)
        return jnp.einsum("bhqij,biqjhd->bqhd", p, v_win)

    out = lax.map(row_block, (jnp.arange(rows), jnp.moveaxis(qg, 1, 0)))
    return jnp.moveaxis(out, 0, 1).reshape(bsz, s, h * dh)


def linear_scan(a, u, reverse):
    def combine(left, right):
        a_l, u_l = left
        a_r, u_r = right
        return a_l * a_r, a_r * u_l + u_r
    _, hs = lax.associative_scan(combine, (a, u), reverse=reverse, axis=1)
    return hs


def rg_lru(x, w_gate, b_gate, lam, reverse):
    bsz, s, _ = x.shape
    xb = x.reshape(bsz, s, N_LRU_BLOCKS, LRU_BLOCK)
    g = jnp.einsum("bsnc,gncd->gbsnd", xb, w_gate).reshape(2, bsz, s, D_LRU)
    g = g.astype(jnp.float32) + b_gate.astype(jnp.float32)[:, None, None, :]
    r_gate = jax.nn.sigmoid(g[0])
    i_gate = jax.nn.sigmoid(g[1])
    log_a = -LRU_C * r_gate * jax.nn.softplus(-lam.astype(jnp.float32))
    a = jnp.exp(log_a)
    u = jnp.sqrt(-jnp.expm1(2.0 * log_a)) * (i_gate * x.astype(jnp.float32))
    return linear_scan(a, u, reverse).astype(x.dtype)


def setup_inputs(seed: int = 0) -> dict:
    key = jax.random.key(seed)
    ks = jax.random.split(key, 20)
    f32 = jnp.float32
    nrm = lambda k, shape, s: jax.random.normal(k, shape, f32) * s
    a_c = jax.random.uniform(ks[8], (DEPTH, 2, D_LRU), f32, 0.9, 0.999)
    a0 = a_c ** (1.0 / LRU_C)
    lru_lam = jnp.log(a0) - jnp.log1p(-a0)
    return {
        "x": nrm(ks[0], (BATCH, SEQ, D_MODEL), 1.0),
        "g_mix": 1.0 + nrm(ks[1], (DEPTH, D_MODEL), 0.02),
        "w_in": nrm(ks[2], (DEPTH, D_MODEL, D_IN), D_MODEL ** -0.5),
        "rel_bias": nrm(ks[3], (DEPTH, N_ATT_HEADS, 2 * WIN_H_MAX - 1, 2 * WIN_W - 1), 0.5),
        "conv_lru_w": nrm(ks[4], (DEPTH, CONV_LRU, D_LRU), CONV_LRU ** -0.5),
        "conv_lru_b": nrm(ks[5], (DEPTH, D_LRU), 0.01),
        "lru_w": nrm(ks[6], (DEPTH, 2, 2, N_LRU_BLOCKS, LRU_BLOCK, LRU_BLOCK), LRU_BLOCK ** -0.5),
        "lru_b": nrm(ks[7], (DEPTH, 2, 2, D_LRU), 0.1),
        "lru_lam": lru_lam,
        "g_att": 1.0 + nrm(ks[9], (DEPTH, D_ATT), 0.02),
        "g_rec": 1.0 + nrm(ks[10], (DEPTH, D_LRU), 0.02),
        "w_out": nrm(ks[11], (DEPTH, D_MIX, D_MODEL), (2.0 * D_MIX) ** -0.5),
        "g_ffn": 1.0 + nrm(ks[12], (DEPTH, D_MODEL), 0.02),
        "w_up": nrm(ks[13], (DEPTH, D_MODEL, 2 * D_FF), D_MODEL ** -0.5),
        "conv_ffn_w": nrm(ks[14], (DEPTH, CONV_FFN, D_FF), CONV_FFN ** -0.5),
        "conv_ffn_b": nrm(ks[15], (DEPTH, D_FF), 0.01),
        "w_down": nrm(ks[16], (DEPTH, D_FF, D_MODEL), (2.0 * D_FF) ** -0.5),
        "g_final": 1.0 + nrm(ks[17], (D_MODEL,), 0.02),
    }


def reference(x, g_mix, w_in, rel_bias, conv_lru_w, conv_lru_b, lru_w, lru_b, lru_lam,
              g_att, g_rec, w_out, g_ffn, w_up, conv_ffn_w, conv_ffn_b, w_down, g_final):
    bsz, s, _ = x.shape
    for l in range(DEPTH):
        h = rmsnorm(x, g_mix[l])
        z = h @ w_in[l]
        q, k, v, x_rec, gate = jnp.split(
            z, [D_ATT, 2 * D_ATT, 3 * D_ATT, 3 * D_ATT + D_LRU], axis=-1)
        att = neighbourhood_attention(
            q.reshape(bsz, s, N_ATT_HEADS, HEAD_DIM),
            k.reshape(bsz, s, N_ATT_HEADS, HEAD_DIM),
            v.reshape(bsz, s, N_ATT_HEADS, HEAD_DIM), rel_bias[l])
        xc = depthwise_conv(x_rec, conv_lru_w[l], conv_lru_b[l], LRU_PAD)
        rec_f = rg_lru(xc, lru_w[l, 0], lru_b[l, 0], lru_lam[l, 0], reverse=False)
        rec_b = rg_lru(xc, lru_w[l, 1], lru_b[l, 1], lru_lam[l, 1], reverse=True)
        rec = (rec_f + rec_b) * jax.nn.gelu(gate, approximate=True)
        mixed = jnp.concatenate([rmsnorm(att, g_att[l]), rmsnorm(rec, g_rec[l])], axis=-1)
        x = x + mixed @ w_out[l]
        h = rmsnorm(x, g_ffn[l])
        u = h @ w_up[l]
        u_act, u_lin = jnp.split(u, 2, axis=-1)
        u_act = depthwise_conv(u_act, conv_ffn_w[l], conv_ffn_b[l], (1, 1))
        x = x + (jax.nn.gelu(u_act, approximate=True) * u_lin) @ w_down[l]
    return rmsnorm(x, g_final)
```

```python
import numpy as np
import concourse.bass as bass
import concourse.mybir as mybir
from concourse.bass_utils import run_bass_kernel_spmd

F32 = mybir.dt.float32
BF16 = mybir.dt.bfloat16
AF = mybir.ActivationFunctionType
ALU = mybir.AluOpType

D = 1024
S = 4096
NT = S // 128
DEPTH = 4
D_ATT = 512
D_LRU = 512
D_IN = 2560
D_FF = 3072
NH = 8
EPS = 1e-6
NEG = -30000.0
NPF = 144
NBLK = 9


class Res:
    __slots__ = ("name", "w", "r")

    def __init__(self, name=""):
        self.name = name
        self.w = None
        self.r = {}


class Op:
    __slots__ = ("eng", "lane", "fn", "deps", "sig", "cnt", "inc", "idx")

    def __init__(self, eng, lane, fn, inc):
        self.idx = 0
        self.eng = eng
        self.lane = lane
        self.fn = fn
        self.deps = []
        self.sig = False
        self.cnt = 0
        self.inc = inc


class Prog:
    ENGS = ("pe", "act", "dve", "pool", "sp")

    def __init__(self):
        self.ops = {e: [] for e in self.ENGS}
        self.lanes = {}
        self.all_ops = []

    def emit(self, eng, fn, reads=(), writes=(), lane=None, deps=()):
        is_dma = lane is not None
        lane = lane if is_dma else eng
        op = Op(eng, lane, fn, 16 if is_dma else 1)
        op.sig = is_dma
        ds = set()
        for r in reads:
            if r.w is not None:
                ds.add(r.w)
        for w in writes:
            if w.w is not None:
                ds.add(w.w)
            for o in w.r.values():
                ds.add(o)
        for d in deps:
            if d is not None:
                ds.add(d)
        ds.discard(op)
        for d in ds:
            if d.lane == "pe" and op.lane == "pe":
                continue
            op.deps.append(d)
            d.sig = True
        for r in reads:
            r.r[lane] = op
        for w in writes:
            w.w = op
            w.r = {}
        op.idx = len(self.all_ops)
        self.ops[eng].append(op)
        self.lanes.setdefault(lane, []).append(op)
        self.all_ops.append(op)
        return op

    def finalize(self):
        for lane, ops in self.lanes.items():
            c = 0
            for op in ops:
                if op.sig:
                    c += op.inc
                    op.cnt = c

    def replay(self, eng, e, sems):
        waited = {}
        for op in self.ops[eng]:
            need = {}
            for d in op.deps:
                if d.cnt > need.get(d.lane, 0):
                    need[d.lane] = d.cnt
            for lane, c in need.items():
                if waited.get(lane, 0) < c:
                    e.wait_ge(sems[lane], c)
                    waited[lane] = c
            ins = op.fn(e)
            if op.sig:
                ins.then_inc(sems[op.lane], op.inc)


def _mb_index():
    GW, WH, WW = 64, 8, 16
    p = np.arange(128)
    ka, kc = p // 64, p % 64
    qb, qc = p // 64, p % 64
    cs = np.clip(qc - WW // 2, 0, GW - WW)
    colok = (kc[:, None] >= cs[None, :]) & (kc[:, None] < cs[None, :] + WW)
    dc = kc[:, None] - qc[None, :] + (WW - 1)
    blocks = []
    specs = [(-2, True), (-1, True), (0, True), (1, True), (2, True),
             (2, False), (3, False), (-2, False), (-3, False)]
    for dt, banded in specs:
        dr_rel = 2 * dt + ka[:, None] - qb[None, :]
        if banded:
            rowok = (dr_rel >= -4) & (dr_rel <= 3)
        else:
            rowok = np.ones_like(dr_rel, bool)
        ok = rowok & colok
        dr = dr_rel + (WH - 1)
        ok &= (dr >= 0) & (dr <= 2 * WH - 2) & (dc >= 0) & (dc <= 2 * WW - 2)
        blocks.append((np.clip(dr, 0, 14), np.clip(dc, 0, 30), ok))
    return blocks


def _prep_shared(inp, depth):
    f32 = np.float32
    blocks = _mb_index()
    rb = np.asarray(inp["rel_bias"], f32)
    mb = np.empty((depth, 128, NH, NBLK, 128), f32)
    for b, (dr, dc, ok) in enumerate(blocks):
        g = rb[:depth][:, :, dr, dc]
        g = np.where(ok[None, None], g, f32(NEG))
        mb[:, :, :, b, :] = g.transpose(0, 2, 1, 3)
    lw = np.asarray(inp["lru_w"], f32)[:depth]
    bd = np.zeros((depth, 128, 2, 2, 4, 128), f32)
    for cc in range(4):
        for hb in range(2):
            bd[:, hb * 64:(hb + 1) * 64, :, :, cc, hb * 64:(hb + 1) * 64] = \
                lw[:, :, :, 2 * cc + hb].transpose(0, 3, 1, 2, 4)
    bd = bd.reshape(depth, 128, 16 * 128)
    pf = np.zeros((depth, 128, NPF), f32)

    def fm(a, nch):
        return a.reshape(a.shape[:-1] + (nch, 128))

    for l in range(depth):
        cw = fm(np.asarray(inp["conv_lru_w"], f32)[l], 4)
        pf[l, :, 0:16] = cw.transpose(2, 1, 0).reshape(128, 16)
        pf[l, :, 16:20] = fm(np.asarray(inp["conv_lru_b"], f32)[l], 4).T
        lb = fm(np.asarray(inp["lru_b"], f32)[l], 4)
        pf[l, :, 20:36] = lb.transpose(3, 0, 1, 2).reshape(128, 16)
        ll = fm(np.asarray(inp["lru_lam"], f32)[l], 4)
        pf[l, :, 36:44] = ll.transpose(2, 0, 1).reshape(128, 8)
        pf[l, :, 44:48] = fm(np.asarray(inp["g_rec"], f32)[l], 4).T
        fw = fm(np.asarray(inp["conv_ffn_w"], f32)[l], 24)
        pf[l, :, 48:120] = fw.transpose(2, 1, 0).reshape(128, 72)
        pf[l, :, 120:144] = fm(np.asarray(inp["conv_ffn_b"], f32)[l], 24).T
    shared = {
        "w_in": np.ascontiguousarray(np.asarray(inp["w_in"], f32)[:depth]),
        "w_out": np.ascontiguousarray(np.asarray(inp["w_out"], f32)[:depth]),
        "w_up": np.ascontiguousarray(np.asarray(inp["w_up"], f32)[:depth]),
        "w_down": np.ascontiguousarray(np.asarray(inp["w_down"], f32)[:depth]),
        "g_mix": np.ascontiguousarray(np.asarray(inp["g_mix"], f32)[:depth]),
        "g_ffn": np.ascontiguousarray(np.asarray(inp["g_ffn"], f32)[:depth]),
        "g_att": np.ascontiguousarray(np.asarray(inp["g_att"], f32)[:depth]),
        "g_final": np.ascontiguousarray(np.asarray(inp["g_final"], f32)).reshape(1, D),
        "mb": mb.reshape(depth, 128, NH * NBLK * 128),
        "bd": bd,
        "pf": pf,
    }
    return shared


KW = 256
ARENA_KIB = 204


def _mk(fn, *a, **k):
    return lambda e: fn(e, *a, **k)


def build(depth=DEPTH, dbg=False):
    nc = bass.Bass("TRN2", target_bir_lowering=False)
    dt = nc.dram_tensor
    x_d = dt("x", [S, D], F32, kind="ExternalInput").ap()
    w_in_d = dt("w_in", [depth, D, D_IN], F32, kind="ExternalInput").ap()
    w_out_d = dt("w_out", [depth, D, D], F32, kind="ExternalInput").ap()
    w_up_d = dt("w_up", [depth, D, 2 * D_FF], F32, kind="ExternalInput").ap()
    w_down_d = dt("w_down", [depth, D_FF, D], F32, kind="ExternalInput").ap()
    g_mix_d = dt("g_mix", [depth, D], F32, kind="ExternalInput").ap()
    g_ffn_d = dt("g_ffn", [depth, D], F32, kind="ExternalInput").ap()
    g_att_d = dt("g_att", [depth, D_ATT], F32, kind="ExternalInput").ap()
    g_fin_d = dt("g_final", [1, D], F32, kind="ExternalInput").ap()
    mb_d = dt("mb", [depth, 128, NH * NBLK * 128], F32, kind="ExternalInput").ap()
    bd_d = dt("bd", [depth, 128, 16 * 128], F32, kind="ExternalInput").ap()
    pf_d = dt("pf", [depth, 128, NPF], F32, kind="ExternalInput").ap()
    idn_d = dt("idn", [128, 128], F32, kind="ExternalInput").ap()
    y_d = dt("y", [S, D], F32, kind="ExternalOutput").ap()
    xs_d = dt("xs", [S, D], F32, kind="Internal").ap()
    zr_d = dt("zr", [1024, S], F32, kind="Internal").ap()
    dbg_t = {}
    if dbg:
        dbg_t["hT"] = dt("d_hT", [128, 8 * S], BF16, kind="ExternalOutput").ap()
        dbg_t["qT"] = dt("d_qT", [128, 4 * S], BF16, kind="ExternalOutput").ap()
        dbg_t["kT"] = dt("d_kT", [128, 4 * S], BF16, kind="ExternalOutput").ap()
        dbg_t["V"] = dt("d_V", [128, NT * NH * 65], BF16, kind="ExternalOutput").ap()
        dbg_t["attT"] = dt("d_attT", [128, 4 * S], BF16, kind="ExternalOutput").ap()
        dbg_t["recT"] = dt("d_recT", [128, 4 * S], BF16, kind="ExternalOutput").ap()
        dbg_t["ssa"] = dt("d_ssa", [128, 64], F32, kind="ExternalOutput").ap()
        dbg_t["zr"] = dt("d_zr", [1024, S], F32, kind="ExternalOutput").ap()
        dbg_t["xD"] = dt("d_xD", [S, D], F32, kind="ExternalOutput").ap()

    P = Prog()
    ctx = nc.sbuf_tensor("arena", [128, ARENA_KIB * KW], F32)
    arena = ctx.__enter__()
    psctx = nc.psum_tensor("psall", [128, 4096], F32)
    ps_all = psctx.__enter__()
    ps = [ps_all[:, b * 512:(b + 1) * 512] for b in range(8)]
    psr = [Res("ps%d" % b) for b in range(8)]

    def reg(off_w, n_w, dtype=F32):
        a = arena[:, off_w:off_w + n_w]
        return a.bitcast(BF16) if dtype == BF16 else a

    def kib(k):
        return int(round(k * KW))

    ident = reg(0, 64, BF16)
    ones = reg(64, 1)
    pf_all = reg(72, depth * NPF).rearrange("p (l f) -> p l f", l=depth)
    drv = reg(648, 64)
    nch = drv[:, 0:8]
    bh = drv[:, 8:24]
    cwh = drv[:, 24:44]
    nch2 = drv[:, 44:52]
    ser = reg(712, 64)
    stat = reg(776, 64)
    ss_att = reg(840, 32)
    rstd_att = reg(872, 32)
    rstd_rec = reg(904, 32)
    gA = reg(1024, 1024)
    gB = reg(2048, 512)
    r_ident, r_ones, r_pf, r_drv, r_gA, r_gB = (Res(n) for n in ("ident", "ones", "pf", "drv", "gA", "gB"))
    r_ssatt, r_rstda, r_rstdr = Res("ssatt"), Res("rstda"), Res("rstdr")

    O_Q, O_K, O_V, O_X1, O_AT = kib(10), kib(42), kib(74), kib(106.5), kib(170.5)
    qT = reg(O_Q, kib(32), BF16).rearrange("p (c t) -> p c t", c=4)
    kT = reg(O_K, kib(32), BF16).rearrange("p (c t) -> p c t", c=4)
    Vt = reg(O_V, NT * NH * 65 // 2, BF16).rearrange("p (t h d) -> p t h d", t=NT, h=NH)
    hT = reg(O_X1, kib(64), BF16).rearrange("p (k t) -> p k t", k=8)
    attT = reg(O_AT, kib(32), BF16).rearrange("p (c t) -> p c t", c=4)
    recT = reg(O_Q, kib(32), BF16).rearrange("p (c t) -> p c t", c=4)

    def fence(res_list):
        f = {}
        for r in res_list:
            for o in ([r.w] if r.w is not None else []) + list(r.r.values()):
                if o.lane not in f or o.idx > f[o.lane].idx:
                    f[o.lane] = o
        return f

    def fenced(name, f):
        r = Res(name)
        r.r = dict(f)
        return r

    users = {rg: [] for rg in ("Q", "K", "V", "X1", "AT")}

    def phase_begin(regs):
        f = {}
        for rg in regs:
            for ln_, o in fence(users[rg]).items():
                if ln_ not in f or o.idx > f[ln_].idx:
                    f[ln_] = o
            users[rg] = []
        return f

    def newres(name, f, regs):
        r = fenced(name, f)
        for rg in regs:
            users[rg].append(r)
        return r

    lane_names = []

    def lane(name):
        if name not in lane_names:
            lane_names.append(name)
        return name

    def act(fn_, out, in_, reads, writes, **kw):
        return P.emit("act", lambda e: e.activation(out=out, in_=in_, func=fn_, **kw), reads, writes)

    def dve(method, reads, writes, **kw):
        return P.emit("dve", lambda e: getattr(e, method)(**kw), reads, writes)

    def pool(method, reads, writes, **kw):
        return P.emit("pool", lambda e: getattr(e, method)(**kw), reads, writes)

    def mm(out, lhsT, rhs, start, stop, reads, writes):
        return P.emit("pe", lambda e: e.matmul(out, lhsT, rhs, start=start, stop=stop), reads, writes)

    def tr(out, in_, reads, writes):
        return P.emit("pe", lambda e: e.transpose(out=out, in_=in_, identity=ident), reads + [r_ident], writes)

    last_pool_dma = [None]

    def dma(eng, ln, out, in_, reads, writes):
        if eng == "pool":
            op = P.emit(eng, lambda e: e.dma_start(out=out, in_=in_), reads, writes, lane=lane(ln),
                        deps=[last_pool_dma[0]])
            last_pool_dma[0] = op
            return op
        return P.emit(eng, lambda e: e.dma_start(out=out, in_=in_), reads, writes, lane=lane(ln))

    def rstd_from_ss(ss_ap, out_ap, n, r_in, r_out, tmp_ap, r_tmp):
        act(AF.Ln, tmp_ap, ss_ap, [r_in], [r_tmp], scale=1.0 / n, bias=EPS)
        act(AF.Exp, out_ap, tmp_ap, [r_tmp], [r_out], scale=-0.5)

    dma("pool", "setup_p", ident, idn_d, [], [r_ident])
    dve("memset", [], [r_ones], ap=ones, constant=1.0)
    dma("sp", "setup", pf_all, pf_d.rearrange("l p f -> p l f"), [], [r_pf])

    r_xs = [Res("xs%d" % i) for i in range(NT)]
    r_zr = [Res("zr%d" % i) for i in range(8)]
    x_tiles = x_d.rearrange("(t p) d -> t p d", p=128)
    xs_tiles = xs_d.rearrange("(t p) d -> t p d", p=128)
    y_tiles = y_d.rearrange("(t p) d -> t p d", p=128)
    out_ops = []

    for l in range(depth):
        src_tiles = x_tiles if l == 0 else xs_tiles
        last = (l == depth - 1)
        dma("sp", "gA", gA, g_mix_d[l:l + 1, :].partition_broadcast(128), [], [r_gA])
        dma("sp", "gB", gB, g_att_d[l:l + 1, :].partition_broadcast(128), [], [r_gB])
        lam = pf_all[:, l, 36:44]
        s_ax, s_y, s_t, s_z, s_z2, s_p, s_m = (ser[:, 8 * i:8 * i + 8] for i in range(7))
        r_s = [Res("ser%d" % i) for i in range(7)]
        act(AF.Abs, s_ax, lam, [r_pf], [r_s[0]])
        act(AF.Exp, s_y, s_ax, [r_s[0]], [r_s[1]], scale=-1.0)
        dve("tensor_scalar", [r_s[1]], [r_s[2]], out=s_t, in0=s_y, scalar1=2.0, scalar2=None, op0=ALU.add)
        dve("reciprocal", [r_s[2]], [r_s[2]], out=s_t, in_=s_t)
        dve("tensor_tensor", [r_s[1], r_s[2]], [r_s[3]], out=s_z, in0=s_y, in1=s_t, op=ALU.mult)
        dve("tensor_tensor", [r_s[3]], [r_s[4]], out=s_z2, in0=s_z, in1=s_z, op=ALU.mult)
        dve("tensor_scalar", [r_s[4]], [r_s[5]], out=s_p, in0=s_z2, scalar1=1.0 / 9, scalar2=1.0 / 7,
            op0=ALU.mult, op1=ALU.add)
        for cst in (1.0 / 5, 1.0 / 3, 1.0):
            dve("tensor_tensor", [r_s[5], r_s[4]], [r_s[5]], out=s_p, in0=s_p, in1=s_z2, op=ALU.mult)
            dve("tensor_scalar", [r_s[5]], [r_s[5]], out=s_p, in0=s_p, scalar1=cst, scalar2=None, op0=ALU.add)
        dve("scalar_tensor_tensor", [r_s[3], r_s[5]], [r_s[5]], out=s_p, in0=s_z, scalar=2.0, in1=s_p,
            op0=ALU.mult, op1=ALU.mult)
        act(AF.Relu, s_m, lam, [r_pf], [r_s[6]], scale=-1.0)
        dve("tensor_tensor", [r_s[5], r_s[6]], [r_s[6]], out=s_m, in0=s_m, in1=s_p, op=ALU.add)
        dve("tensor_scalar", [r_s[6]], [r_drv], out=nch, in0=s_m, scalar1=-4.0, scalar2=None, op0=ALU.mult)
        dve("tensor_scalar", [r_s[6]], [r_drv], out=nch2, in0=s_m, scalar1=-8.0, scalar2=None, op0=ALU.mult)
        dve("tensor_scalar", [r_pf], [r_drv], out=bh, in0=pf_all[:, l, 20:36], scalar1=0.5, scalar2=None,
            op0=ALU.mult)
        dve("tensor_scalar", [r_pf], [r_drv], out=cwh, in0=pf_all[:, l, 0:20], scalar1=0.5, scalar2=None,
            op0=ALU.mult)

        O_W = O_AT
        xt = [reg(O_W + kib(4) * i, kib(4)) for i in range(4)]
        ht = [reg(O_W + kib(16) + kib(2) * i, kib(2), BF16) for i in range(3)]
        junk = reg(O_W + kib(22), kib(2), BF16)
        fA = phase_begin(["Q", "K", "V", "X1", "AT"])
        r_xt = [newres("xt%d" % i, fA, ["AT"]) for i in range(4)]
        r_ht = [newres("ht%d" % i, fA, ["AT"]) for i in range(3)]
        r_junk = newres("junk", fA, ["AT"])
        r_st = [Res("st%d" % i) for i in range(4)]
        r_hT = [newres("hT%d" % i, fA, ["X1"]) for i in range(NT)]
        def A0_S0(i):
            sl = i % 4
            dma("sp", "xt%d" % sl, xt[sl], src_tiles[i], [r_xs[i]], [r_xt[sl]])

        def A0_S1(i):
            sl, s2 = i % 4, i % 2
            ssx, tmpx, rsx = stat[:, 4 * s2:4 * s2 + 1], stat[:, 4 * s2 + 1:4 * s2 + 2], stat[:, 4 * s2 + 2:4 * s2 + 3]
            act(AF.Square, junk, xt[sl], [r_xt[sl]], [r_junk, r_st[s2]], accum_out=ssx)
            rstd_from_ss(ssx, rsx, D, r_st[s2], r_st[s2], tmpx, r_st[s2])

        def A0_S2(i):
            sl, hl, s2 = i % 4, i % 3, i % 2
            rsx = stat[:, 4 * s2 + 2:4 * s2 + 3]
            dve("scalar_tensor_tensor", [r_xt[sl], r_st[s2], r_gA], [r_ht[hl]], out=ht[hl], in0=xt[sl],
                scalar=rsx, in1=gA, op0=ALU.mult, op1=ALU.mult)

        def A0_S3(i):
            hl, pb = i % 3, i % 2
            pst = ps[pb].bitcast(BF16)
            for k in range(8):
                tr(pst[:, k * 128:(k + 1) * 128], ht[hl][:, k * 128:(k + 1) * 128], [r_ht[hl]], [psr[pb]])

        def A0_S4(i):
            pb = i % 2
            pst = ps[pb].bitcast(BF16)
            o = hT[:, :, i * 128:(i + 1) * 128]
            s_ = pst.rearrange("p (k t) -> p k t", k=8)
            if i % 2 == 0:
                act(AF.Copy, o, s_, [psr[pb]], [r_hT[i]])
            else:
                dve("tensor_copy", [psr[pb]], [r_hT[i]], out=o, in_=s_)

        wb = [reg(O_W + kib(24), kib(8), BF16).rearrange("p (k c) -> p k c", k=8), None]
        r_wb = [newres("wb0", fA, ["AT"]), None]
        r_qT = [newres("qT%d" % i, fA, ["Q"]) for i in range(8)]
        r_kT = [newres("kT%d" % i, fA, ["K"]) for i in range(8)]
        r_V = [newres("V%d" % i, fA, ["V"]) for i in range(NT)]
        r_Vones = newres("Vones", fA, ["V"])
        pool("memset", [], [r_Vones], ap=Vt[:, :, :, 64:65], constant=1.0)
        w_l = w_in_d[l].rearrange("(k p) c -> p k c", p=128)
        a1s = {"nev": 0, "bank": 0, "bk0": 0}
        dma("pool", "wb0", wb[0], w_l[:, :, 0:512], [], [r_wb[0]])

        def a1_q_chunk(tc):
            for c in range(4):
                b = 2 + a1s["bk0"] % 6
                a1s["bk0"] += 1
                rd = [r_hT[4 * tc + j] for j in range(4)] + [r_wb[0]]
                for k in range(8):
                    mm(ps[b], wb[0][:, k, c * 128:(c + 1) * 128], hT[:, k, tc * 512:(tc + 1) * 512],
                       k == 0, k == 7, rd, [psr[b]])
                o = qT[:, c, tc * 512:(tc + 1) * 512]
                if a1s["nev"] % 2 == 0:
                    act(AF.Copy, o, ps[b], [psr[b]], [r_qT[tc]], scale=0.125)
                else:
                    dve("tensor_scalar", [psr[b]], [r_qT[tc]], out=o, in0=ps[b], scalar1=0.125, scalar2=None,
                        op0=ALU.mult)
                a1s["nev"] += 1

        for i_ in range(3):
            A0_S0(i_)
        for t_ in range(NT + 3):
            if t_ < NT:
                A0_S1(t_)
            if 0 <= t_ - 1 < NT:
                A0_S2(t_ - 1)
            if t_ + 3 < NT:
                A0_S0(t_ + 3)
            if 0 <= t_ - 2 < NT:
                A0_S3(t_ - 2)
            if 0 <= t_ - 3 < NT:
                A0_S4(t_ - 3)
                if (t_ - 3) % 4 == 3:
                    a1_q_chunk((t_ - 3) // 4)
        if dbg and l == 0:
            out_ops.append(dma("sp", "dbgx1", dbg_t["hT"], hT.rearrange("p k t -> p (k t)"), r_hT, []))

        f_w = fence(r_xt + r_ht + [r_junk])
        wb[1] = reg(O_W, kib(8), BF16).rearrange("p (k c) -> p k c", k=8)
        r_wb[1] = newres("wb1", f_w, ["AT"])
        stg = [reg(O_W + kib(8) + kib(2) * i, kib(2)) for i in range(4)]
        r_stg = [newres("stg%d" % i, f_w, ["AT"]) for i in range(4)]
        stg_i = 0
        for g in range(1, 5):
            sl = g % 2
            dma("pool", "wb%d" % sl, wb[sl], w_l[:, :, g * 512:(g + 1) * 512], [], [r_wb[sl]])
            if g == 2:
                for i in range(NT):
                    b = a1s["bank"] % 8
                    a1s["bank"] += 1
                    for k in range(8):
                        mm(ps[b], hT[:, k, i * 128:(i + 1) * 128], wb[sl][:, k, :], k == 0, k == 7,
                           [r_hT[i], r_wb[sl]], [psr[b]])
                    o = Vt[:, i, :, 0:64]
                    s_ = ps[b].rearrange("p (h d) -> p h d", h=NH)
                    if a1s["nev"] % 2 == 0:
                        act(AF.Copy, o, s_, [psr[b]], [r_V[i]])
                    else:
                        dve("tensor_copy", [psr[b]], [r_V[i]], out=o, in_=s_)
                    a1s["nev"] += 1
                continue
            for c in range(4):
                for tc in range(8):
                    b = a1s["bank"] % 8
                    a1s["bank"] += 1
                    rd = [r_hT[4 * tc + j] for j in range(4)] + [r_wb[sl]]
                    for k in range(8):
                        mm(ps[b], wb[sl][:, k, c * 128:(c + 1) * 128], hT[:, k, tc * 512:(tc + 1) * 512],
                           k == 0, k == 7, rd, [psr[b]])
                    if g == 1:
                        o = kT[:, c, tc * 512:(tc + 1) * 512]
                        if a1s["nev"] % 2 == 0:
                            act(AF.Copy, o, ps[b], [psr[b]], [r_kT[tc]])
                        else:
                            dve("tensor_copy", [psr[b]], [r_kT[tc]], out=o, in_=ps[b])
                    else:
                        ss_ = stg_i % 4
                        stg_i += 1
                        if a1s["nev"] % 2 == 0:
                            act(AF.Copy, stg[ss_], ps[b], [psr[b]], [r_stg[ss_]])
                        else:
                            dve("tensor_copy", [psr[b]], [r_stg[ss_]], out=stg[ss_], in_=ps[b])
                        row = (g - 3) * 4 + c
                        dma("sp", "stg%d" % ss_, zr_d[row * 128:(row + 1) * 128, tc * 512:(tc + 1) * 512], stg[ss_],
                            [r_stg[ss_]], [r_zr[row]])
                    a1s["nev"] += 1
        if dbg and l == 0:
            out_ops.append(dma("sp", "dbgx2", dbg_t["qT"], qT.rearrange("p c t -> p (c t)"), r_qT, []))
            out_ops.append(dma("sp", "dbgx3", dbg_t["kT"], kT.rearrange("p c t -> p (c t)"), r_kT, []))
            out_ops.append(dma("sp", "dbgx4", dbg_t["V"], Vt.rearrange("p t h d -> p (t h d)"), r_V + [r_Vones], []))
            out_ops.append(dma("sp", "dbgx5", dbg_t["zr"], zr_d, r_zr, []))
        if dbg == "A":
            break

        fB = phase_begin(["X1", "AT"])
        mbt = reg(O_X1, kib(18), BF16).rearrange("p (h b q) -> p h b q", h=NH, b=NBLK)
        r_mb = newres("mb", fB, ["X1"])
        dma("pool", "mb", reg(O_X1, kib(18), BF16), mb_d[l], [], [r_mb])
        ob = O_X1 + kib(18)
        qz = [reg(ob + 512 * i, 512, BF16).rearrange("p (h q) -> p h q", h=NH) for i in range(2)]
        pTb = [reg(ob + 1024 + 320 * i, 320, BF16) for i in range(2)]
        attb = [reg(ob + 1920 + 512 * i, 512) for i in range(2)]
        attn = [reg(ob + 2944 + 256 * i, 256, BF16) for i in range(2)]
        junkb = reg(ob + 3456, 256, BF16)
        rsb = reg(ob + 3712, 16)
        r_qz = [newres("qz%d" % i, fB, ["X1"]) for i in range(2)]
        r_pTb = [newres("pTb%d" % i, fB, ["X1"]) for i in range(2)]
        r_attb = [newres("attb%d" % i, fB, ["X1"]) for i in range(2)]
        r_attn = [newres("attn%d" % i, fB, ["X1"]) for i in range(2)]
        r_junkb = newres("junkb", fB, ["X1"])
        r_rsb = [newres("rsb%d" % i, fB, ["X1"]) for i in range(2)]
        r_attT = [newres("attT%d" % i, fB, ["AT"]) for i in range(NT)]
        for i in range(2):
            pool("memset", [], [r_qz[i]], ap=qz[i], constant=0.0)

        def tile_plan(j):
            if j == 0:
                return [0, 1, 2, 3], [2, 3, 5, 6]
            if j == 1:
                return [0, 1, 2, 3], [1, 2, 3, 5]
            if j == NT - 2:
                return [28, 29, 30, 31], [7, 1, 2, 3]
            if j == NT - 1:
                return [28, 29, 30, 31], [8, 7, 1, 2]
            return [j - 2, j - 1, j, j + 1, j + 2], [0, 1, 2, 3, 4]

        def runs(bl):
            out, st = [], 0
            for i in range(1, len(bl) + 1):
                if i == len(bl) or bl[i] != bl[i - 1] + 1:
                    out.append((st, i))
                    st = i
            return out

        units = [(j, h) for j in range(NT) for h in range(NH)]

        def spv(s):
            return ps_all[:, s * 1024:s * 1024 + 1024].rearrange("p (n q) -> p n q", n=8)

        def r_sp(s):
            return [psr[2 * s], psr[2 * s + 1]]

        def emit_qz(j):
            qs = j % 2
            ts_ = slice(j * 128, (j + 1) * 128)
            pool("tensor_copy", [r_qT[j // 4]], [r_qz[qs]], out=qz[qs][0:64, 0::2, :], in_=qT[0:64, :, ts_])
            pool("tensor_copy", [r_qT[j // 4]], [r_qz[qs]], out=qz[qs][64:128, 1::2, :], in_=qT[64:128, :, ts_])

        def emit_qk(u):
            j, h = units[u]
            tiles, blks = tile_plan(j)
            s = u % 2
            c = h // 2
            if u == 0:
                emit_qz(0)
            if h == 3 and j + 1 < NT:
                emit_qz(j + 1)
            segs = []
            for (a_, e_) in runs(blks):
                if a_ < 4 < e_:
                    segs += [(a_, 4), (4, e_)]
                else:
                    segs.append((a_, e_))
            for (a_, e_) in segs:
                pr_ = [psr[2 * s + a_ // 4]]
                mm(spv(s)[:, a_:e_, :].rearrange("p n q -> p (n q)"), ident,
                   mbt[:, h, blks[a_]:blks[a_] + (e_ - a_), :].rearrange("p n q -> p (n q)"), True, False,
                   [r_ident, r_mb], pr_)
                for n in range(a_, e_):
                    t = tiles[n]
                    mm(spv(s)[:, n, :], kT[:, c, t * 128:(t + 1) * 128], qz[j % 2][:, h, :], False, n == e_ - 1,
                       [r_kT[t // 4], r_qz[j % 2]], pr_)

        def emit_exp(u):
            j, h = units[u]
            tiles, blks = tile_plan(j)
            nb = len(tiles)
            s = u % 2
            act(AF.Exp, pTb[s][:, 0:nb * 128], ps_all[:, s * 1024:s * 1024 + nb * 128],
                [psr[2 * s], psr[2 * s + 1]], [r_pTb[s]])

        def emit_pv(u):
            j, h = units[u]
            tiles, blks = tile_plan(j)
            nb = len(tiles)
            s = u % 2
            ob_ = 4 + h // 4
            for n, t in enumerate(tiles):
                mm(ps[ob_][:, (h % 4) * 128:(h % 4) * 128 + 65], pTb[s][:, n * 128:(n + 1) * 128], Vt[:, t, h, :],
                   n == 0, n == nb - 1, [r_pTb[s], r_V[t], r_Vones], [psr[ob_]])
            if h % 4 == 3:
                hf = h // 4
                sl = j % 2
                opv = ps[ob_].rearrange("p (g d) -> p g d", g=4)
                dve("reciprocal", [psr[ob_]], [r_rsb[hf]], out=rsb[:, 4 * hf:4 * hf + 4], in_=opv[:, :, 64])
                dve("tensor_tensor", [psr[ob_], r_rsb[hf]], [r_attb[sl]],
                    out=attb[sl][:, hf * 256:(hf + 1) * 256].rearrange("p (g d) -> p g d", g=4),
                    in0=opv[:, :, 0:64], in1=rsb[:, 4 * hf:4 * hf + 4].unsqueeze(2).to_broadcast([128, 4, 64]),
                    op=ALU.mult)

        def emit_post(j):
            sl = j % 2
            act(AF.Square, junkb, attb[sl], [r_attb[sl]], [r_junkb, r_ssatt], accum_out=ss_att[:, j:j + 1])
            dve("tensor_tensor", [r_attb[sl], r_gB], [r_attn[sl]], out=attn[sl], in0=attb[sl], in1=gB, op=ALU.mult)

        def emit_tr(j):
            sl = j % 2
            pb = 6 + j % 2
            pst = ps[pb].bitcast(BF16)
            for k in range(4):
                tr(pst[:, k * 128:(k + 1) * 128], attn[sl][:, k * 128:(k + 1) * 128], [r_attn[sl]], [psr[pb]])
            act(AF.Copy, attT[:, :, j * 128:(j + 1) * 128], pst[:, 0:512].rearrange("p (k t) -> p k t", k=4),
                [psr[pb]], [r_attT[j]])

        def after_pv(u):
            j, h = units[u]
            if h == NH - 1:
                emit_post(j)
            if h == 1 and j > 0:
                emit_tr(j - 1)

        emit_qk(0)
        for u in range(len(units)):
            if u + 1 < len(units):
                emit_qk(u + 1)
            emit_exp(u)
            if u >= 1:
                emit_pv(u - 1)
                after_pv(u - 1)
        emit_pv(len(units) - 1)
        after_pv(len(units) - 1)
        emit_tr(NT - 1)
        if dbg and l == 0:
            out_ops.append(dma("sp", "dbgy1", dbg_t["attT"], attT.rearrange("p c t -> p (c t)"), r_attT, []))
            out_ops.append(dma("sp", "dbgy2", dbg_t["ssa"][:, 0:32], ss_att, [r_ssatt], []))
        if dbg == "B":
            break

        fC = phase_begin(["Q", "K", "V", "X1"])
        o = O_K
        xr = reg(o, 4104); o += 4104
        xch = reg(o, 4096); o += 4096
        xcb = reg(o, 2048, BF16); o += 2048
        Hf = reg(o, 4096); o += 4096
        Ab = reg(o, 4096); o += 4096
        Ib = reg(o, 4096); o += 4096
        Tb = reg(o, 4096); o += 4096
        Hb1 = reg(o, 2048); o += 2048
        sqb = [reg(o + 512 * i, 512) for i in range(2)]; o += 1024
        bdt = reg(o, 1024, BF16).rearrange("p (m c) -> p m c", m=16); o += 1024
        assert o <= O_AT
        gate = xr[:, 2:4098]
        LR = ["K", "V", "X1"]
        r_xr = newres("xr", fC, LR)
        r_xrpad = newres("xrpad", fC, LR)
        r_xch = [newres("xch%d" % i, fC, LR) for i in range(2)]
        r_xcb = [newres("xcb%d" % i, fC, LR) for i in range(2)]
        r_Hf = [newres("Hf%d" % i, fC, LR) for i in range(8)]
        r_A = [newres("A%d" % i, fC, LR) for i in range(8)]
        r_I = [newres("I%d" % i, fC, LR) for i in range(8)]
        r_T = [newres("T%d" % i, fC, LR) for i in range(2)]
        r_Hb1 = newres("Hb1", fC, LR)
        r_sqb = [newres("sqb%d" % i, fC, LR) for i in range(2)]
        r_bd = newres("bd", fC, LR)
        r_recT = [newres("recT%d" % i, fC, ["Q"]) for i in range(8)]
        dma("pool", "bd", bdt.rearrange("p m c -> p (m c)"), bd_d[l], [], [r_bd])
        pool("memset", [], [r_xrpad], ap=xr[:, 0:2], constant=0.0)
        pool("memset", [], [r_xrpad], ap=xr[:, 4098:4104], constant=0.0)
        pfl = pf_all[:, l, :]
        gcnt = 0
        for cc in range(4):
            dma("sp", "xr", xr[:, 2:4098], zr_d[cc * 128:(cc + 1) * 128, :], [r_zr[cc]], [r_xr])
            for hf in range(2):
                c0 = hf * 2048
                act(AF.Identity, xch[:, c0:c0 + 2048], xr[:, 2 + c0:2 + c0 + 2048], [r_xr, r_xrpad, r_drv], [r_xch[hf]],
                    scale=cwh[:, cc * 4 + 2:cc * 4 + 3], bias=cwh[:, 16 + cc:17 + cc])
            for hf in range(2):
                c0 = hf * 2048
                dst = xch[:, c0:c0 + 2048]
                for tap in (0, 1, 3):
                    dve("scalar_tensor_tensor", [r_xr, r_xrpad, r_drv, r_xch[hf]], [r_xch[hf]], out=dst,
                        in0=xr[:, tap + c0:tap + c0 + 2048], scalar=cwh[:, cc * 4 + tap:cc * 4 + tap + 1], in1=dst,
                        op0=ALU.mult, op1=ALU.add)
            for hf in range(2):
                c0 = hf * 2048
                act(AF.Copy, xcb[:, c0:c0 + 2048], xch[:, c0:c0 + 2048], [r_xch[hf]], [r_xcb[hf]], scale=2.0)
            dma("sp", "xr", gate, zr_d[512 + cc * 128:512 + (cc + 1) * 128, :], [r_zr[4 + cc]], [r_xr])
            for stage, items in enumerate([[(0, 0), (1, 1)], [(0, 1), (1, 0)]]):
                for (d_, hf) in items:
                    for tc in range(4 * hf, 4 * hf + 4):
                        s = gcnt % 2
                        gcnt += 1
                        cs = slice(tc * 512, (tc + 1) * 512)
                        mm(ps[2 * s], bdt[:, d_ * 8 + cc, :], xcb[:, cs], True, True, [r_bd, r_xcb[hf]], [psr[2 * s]])
                        mm(ps[2 * s + 1], bdt[:, d_ * 8 + 4 + cc, :], xcb[:, cs], True, True, [r_bd, r_xcb[hf]],
                           [psr[2 * s + 1]])
                        act(AF.Tanh, Ab[:, cs], ps[2 * s], [psr[2 * s], r_drv], [r_A[tc]], scale=0.5,
                            bias=bh[:, d_ * 8 + cc:d_ * 8 + cc + 1])
                        act(AF.Tanh, Ib[:, cs], ps[2 * s + 1], [psr[2 * s + 1], r_drv], [r_I[tc]], scale=0.5,
                            bias=bh[:, d_ * 8 + 4 + cc:d_ * 8 + 5 + cc])
                for (d_, hf) in items:
                    hs = slice(hf * 2048, hf * 2048 + 2048)
                    ncol = nch[:, d_ * 4 + cc:d_ * 4 + cc + 1]
                    ncol2 = nch2[:, d_ * 4 + cc:d_ * 4 + cc + 1]
                    rA, rI = r_A[4 * hf:4 * hf + 4], r_I[4 * hf:4 * hf + 4]
                    act(AF.Exp, Tb[:, hs], Ab[:, hs], rA + [r_drv], [r_T[hf]], scale=ncol2, bias=ncol2)
                    act(AF.Exp, Ab[:, hs], Ab[:, hs], rA + [r_drv], rA, scale=ncol, bias=ncol)
                    dve("scalar_tensor_tensor", rI + [r_xch[hf]], rI, out=Ib[:, hs], in0=Ib[:, hs], scalar=1.0,
                        in1=xch[:, hs], op0=ALU.add, op1=ALU.mult)
                for (d_, hf) in items:
                    hs = slice(hf * 2048, hf * 2048 + 2048)
                    rI = r_I[4 * hf:4 * hf + 4]
                    act(AF.Sqrt, Tb[:, hs], Tb[:, hs], [r_T[hf]], [r_T[hf]], scale=-1.0, bias=1.0)
                    dve("tensor_tensor", rI + [r_T[hf]], rI, out=Ib[:, hs], in0=Ib[:, hs], in1=Tb[:, hs], op=ALU.mult)
                for (d_, hf) in items:
                    hs = slice(hf * 2048, hf * 2048 + 2048)
                    rA, rI = r_A[4 * hf:4 * hf + 4], r_I[4 * hf:4 * hf + 4]
                    if d_ == 0:
                        init = 0.0 if hf == 0 else Hf[:, 2047:2048]
                        dve("tensor_tensor_scan", rA + rI + (r_Hf[0:4] if hf == 1 else []), r_Hf[4 * hf:4 * hf + 4],
                            out=Hf[:, hs], data0=Ab[:, hs], data1=Ib[:, hs], initial=init, op0=ALU.mult, op1=ALU.add)
                    elif hf == 1:
                        dve("tensor_tensor_scan", rA + rI, [r_Hb1], out=Hb1[:, ::-1], data0=Ab[:, hs][:, ::-1],
                            data1=Ib[:, hs][:, ::-1], initial=0.0, op0=ALU.mult, op1=ALU.add)
                    else:
                        dve("tensor_tensor_scan", rA + rI + [r_Hb1, r_T[0]], [r_T[0]], out=Tb[:, 0:2048][:, ::-1],
                            data0=Ab[:, hs][:, ::-1], data1=Ib[:, hs][:, ::-1], initial=Hb1[:, 0:1],
                            op0=ALU.mult, op1=ALU.add)
                if stage == 0:
                    act(AF.Gelu_apprx_tanh, gate, gate, [r_xr], [r_xr])
            for hf in (1, 0):
                hs = slice(hf * 2048, hf * 2048 + 2048)
                rH = r_Hf[4 * hf:4 * hf + 4]
                hbsrc, rhb = (Hb1, r_Hb1) if hf == 1 else (Tb[:, 0:2048], r_T[0])
                dve("tensor_tensor", rH + [rhb], rH, out=Hf[:, hs], in0=Hf[:, hs], in1=hbsrc, op=ALU.add)
                dve("tensor_tensor", rH + [r_xr], rH, out=Hf[:, hs], in0=Hf[:, hs], in1=gate[:, hs], op=ALU.mult)
                act(AF.Identity, recT[:, cc, hs], Hf[:, hs], rH + [r_pf], r_recT[4 * hf:4 * hf + 4],
                    scale=pfl[:, 44 + cc:45 + cc])
                for tc in range(4 * hf, 4 * hf + 4):
                    s = tc % 2
                    cs = slice(tc * 512, (tc + 1) * 512)
                    act(AF.Square, sqb[s], Hf[:, cs], [r_Hf[tc]], [r_sqb[s]])
                    for ti in range(4):
                        tok = tc * 4 + ti
                        col = cc * 32 + tok
                        mm(ps[4][:, col:col + 1], sqb[s][:, ti * 128:(ti + 1) * 128], ones, True, True,
                           [r_sqb[s], r_ones], [psr[4]])
        if dbg and l == 0:
            out_ops.append(dma("sp", "dbgy3", dbg_t["recT"], recT.rearrange("p c t -> p (c t)"), r_recT, []))
        if dbg == "C":
            break

        fD = phase_begin(["K", "V", "X1"])
        o = O_K
        h2T = reg(o, 16392, BF16).rearrange("p (k t) -> p k t", k=8); o += 16392
        wo = reg(o, 4096, BF16).rearrange("p (k c) -> p k c", k=8); o += 4096
        xt = [reg(o + 1024 * i, 1024) for i in range(4)]; o += 4096
        ht = [reg(o + 512 * i, 512, BF16) for i in range(3)]; o += 1536
        junk = reg(o, 512, BF16); o += 512
        tmpP = [reg(o + 1024 * i, 1024) for i in range(2)]; o += 2048
        assert o <= O_AT
        LD = ["K", "V", "X1"]
        r_h2T = [newres("h2T%d" % i, fD, LD) for i in range(NT)]
        r_h2pad = newres("h2pad", fD, LD)
        r_wo = newres("wo", fD, LD)
        r_xt = [newres("xtD%d" % i, fD, LD) for i in range(4)]
        r_ht = [newres("htD%d" % i, fD, LD) for i in range(3)]
        r_junk = newres("junkD", fD, LD)
        r_tmpP = [newres("tmpPD%d" % i, fD, LD) for i in range(2)]
        pool("memset", [], [r_h2pad], ap=h2T[:, :, 0:1], constant=0.0)
        pool("memset", [], [r_h2pad], ap=h2T[:, :, 4097:4098], constant=0.0)
        dma("pool", "wo", wo, w_out_d[l].rearrange("(k p) c -> p k c", p=128), [], [r_wo])
        dma("sp", "gA", gA, g_ffn_d[l:l + 1, :].partition_broadcast(128), [], [r_gA])
        tmpa = stat[:, 16:48]
        r_tmpa = Res("tmpa")
        rstd_from_ss(ss_att, rstd_att, D_ATT, r_ssatt, r_rstda, tmpa, r_tmpa)
        ssr = stat[:, 48:80] if False else reg(936, 32)
        r_ssr = Res("ssr")
        dve("tensor_copy", [psr[4]], [r_ssr], out=ssr, in_=ps[4][:, 0:32])
        for cc_ in range(1, 4):
            dve("tensor_tensor", [psr[4], r_ssr], [r_ssr], out=ssr, in0=ssr, in1=ps[4][:, cc_ * 32:cc_ * 32 + 32], op=ALU.add)
        rstd_from_ss(ssr, rstd_rec, D_LRU, r_ssr, r_rstdr, tmpa, r_tmpa)
        def D_S0(i):
            sl = i % 4
            dma("sp", "xtD%d" % sl, xt[sl], src_tiles[i], [r_xs[i]], [r_xt[sl]])

        def stage1D(i):
            sl, s2 = i % 4, i % 2
            for c in range(2):
                cs = slice(c * 512, (c + 1) * 512)
                for k in range(4):
                    mm(ps[2 * c], attT[:, k, i * 128:(i + 1) * 128], wo[:, k, cs], k == 0, k == 3,
                       [r_attT[i], r_wo], [psr[2 * c]])
                for k in range(4):
                    mm(ps[2 * c + 1], recT[:, k, i * 128:(i + 1) * 128], wo[:, 4 + k, cs], k == 0, k == 3,
                       [r_recT[i // 4], r_wo], [psr[2 * c + 1]])
                dve("scalar_tensor_tensor", [psr[2 * c], r_rstda, r_xt[sl]], [r_xt[sl]], out=xt[sl][:, cs],
                    in0=ps[2 * c], scalar=rstd_att[:, i:i + 1], in1=xt[sl][:, cs], op0=ALU.mult, op1=ALU.add)
                dve("scalar_tensor_tensor", [psr[2 * c + 1], r_rstdr, r_xt[sl]], [r_xt[sl]], out=xt[sl][:, cs],
                    in0=ps[2 * c + 1], scalar=rstd_rec[:, i:i + 1], in1=xt[sl][:, cs], op0=ALU.mult, op1=ALU.add)
            dma("sp", "xsD%d" % sl, xs_tiles[i], xt[sl], [r_xt[sl]], [r_xs[i]])
            ssx, tmpx, rsx = stat[:, 4 * s2:4 * s2 + 1], stat[:, 4 * s2 + 1:4 * s2 + 2], stat[:, 4 * s2 + 2:4 * s2 + 3]
            act(AF.Square, junk, xt[sl], [r_xt[sl]], [r_junk, r_st[s2]], accum_out=ssx)
            rstd_from_ss(ssx, rsx, D, r_st[s2], r_st[s2], tmpx, r_st[s2])

        def D_S2(i):
            sl, hl, s2 = i % 4, i % 3, i % 2
            rsx = stat[:, 4 * s2 + 2:4 * s2 + 3]
            dve("scalar_tensor_tensor", [r_xt[sl], r_st[s2], r_gA], [r_ht[hl]], out=ht[hl], in0=xt[sl],
                scalar=rsx, in1=gA, op0=ALU.mult, op1=ALU.mult)

        def D_S3(i):
            hl, pb = i % 3, 6 + i % 2
            pst = ps[pb].bitcast(BF16)
            for k in range(8):
                tr(pst[:, k * 128:(k + 1) * 128], ht[hl][:, k * 128:(k + 1) * 128], [r_ht[hl]], [psr[pb]])

        def D_S4(i):
            pb = 6 + i % 2
            pst = ps[pb].bitcast(BF16)
            o_ = h2T[:, :, 1 + i * 128:1 + (i + 1) * 128]
            s_ = pst.rearrange("p (k t) -> p k t", k=8)
            act(AF.Copy, o_, s_, [psr[pb]], [r_h2T[i]])

        for i_ in range(3):
            D_S0(i_)
        for t_ in range(NT + 3):
            if t_ < NT:
                stage1D(t_)
            if 0 <= t_ - 1 < NT:
                D_S2(t_ - 1)
            if t_ + 3 < NT:
                D_S0(t_ + 3)
            if 0 <= t_ - 2 < NT:
                D_S3(t_ - 2)
            if 0 <= t_ - 3 < NT:
                D_S4(t_ - 3)
        if dbg and l == 0:
            out_ops.append(dma("sp", "dbgy4", dbg_t["xD"], xs_d, r_xs, []))
        if dbg == "D":
            break

        fE = phase_begin(["X1", "AT", "Q"])
        LE = ["X1", "AT", "Q"]
        xacc = reg(O_X1, 8192).rearrange("p (t d) -> p t d", t=8)
        wbase = O_X1 + 8192
        wua = [reg(wbase + 6144 * i, 2048, BF16).rearrange("p (k c) -> p k c", k=8) for i in range(2)]
        wul = [reg(wbase + 6144 * i + 2048, 2048, BF16).rearrange("p (k c) -> p k c", k=8) for i in range(2)]
        wd = [reg(wbase + 6144 * i + 4096, 2048, BF16).rearrange("p (f c) -> p f c", f=4) for i in range(2)]
        abase = wbase + 12288
        actT = [reg(abase + 2048 * i, 2048, BF16).rearrange("p (f t) -> p f t", f=4) for i in range(2)]
        assert abase + 4096 <= ARENA_KIB * KW
        cbuf = [reg(O_Q + 512 * i, 512) for i in range(2)]
        gbuf = [reg(O_Q + 1024 + 512 * i, 512) for i in range(2)]
        stgE = [reg(O_Q + 2048 + 512 * i, 512) for i in range(2)]
        r_xacc = [newres("xacc%d" % i, fE, LE) for i in range(8)]
        r_wu = [newres("wu%d" % i, fE, LE) for i in range(2)]
        r_wl = [newres("wl%d" % i, fE, LE) for i in range(2)]
        r_wd = [newres("wd%d" % i, fE, LE) for i in range(2)]
        r_actT = [newres("actT%d" % i, fE, LE) for i in range(2)]
        r_cbuf = [newres("cbuf%d" % i, fE, LE) for i in range(2)]
        r_gbuf = [newres("gbuf%d" % i, fE, LE) for i in range(2)]
        r_stgE = [newres("stgE%d" % i, fE, LE) for i in range(2)]
        if last:
            dma("sp", "gA", gA, g_fin_d.partition_broadcast(128), [], [r_gA])
        wup = w_up_d[l].rearrange("(k p) c -> p k c", p=128)
        wdn = w_down_d[l].rearrange("(f p) c -> p f c", p=128)
        groups = [(tcn, G) for tcn in range(4) for G in range(6)]
        if dbg == "E1":
            groups = groups[:1]
        if dbg == "E2":
            groups = groups[:6]
        if dbg == "E3":
            groups = groups[:3]
        ucnt = [0]
        SUBO = [0, 342, 684, 1024]
        pycnt = [0]

        def emit_wload(gi):
            tcn, G = groups[gi]
            ws = gi % 2
            dma("pool", "wua%d" % ws, wua[ws], wup[:, :, G * 512:(G + 1) * 512], [], [r_wu[ws]])
            dma("pool", "wul%d" % ws, wul[ws], wup[:, :, D_FF + G * 512:D_FF + (G + 1) * 512], [], [r_wl[ws]])
            dma("pool", "wd%d" % ws, wd[ws], wdn[:, G * 4:(G + 1) * 4, :], [], [r_wd[ws]])

        def emit_up(gi):
            tcn, G = groups[gi]
            ws = gi % 2
            for f4 in range(4):
                fch = G * 4 + f4
                w0 = pfl[:, 48 + fch * 3:49 + fch * 3]
                w1 = pfl[:, 49 + fch * 3:50 + fch * 3]
                w2 = pfl[:, 50 + fch * 3:51 + fch * 3]
                bb = pfl[:, 120 + fch:121 + fch]
                for sub in range(3):
                    o0, o1 = SUBO[sub], SUBO[sub + 1]
                    sz = o1 - o0
                    s0 = tcn * 1024 + o0
                    us = ucnt[0] % 2
                    ucnt[0] += 1
                    ua, ul = ps[2 * us][:, 0:sz + 2], ps[2 * us + 1][:, 0:sz]
                    tlo, thi = max((s0 - 1) // 128, 0), min((s0 + sz) // 128, NT - 1)
                    rh = [r_h2T[q] for q in range(tlo, thi + 1)] + [r_h2pad]
                    for k in range(8):
                        mm(ua, wua[ws][:, k, f4 * 128:(f4 + 1) * 128], h2T[:, k, s0:s0 + sz + 2], k == 0, k == 7,
                           rh + [r_wu[ws]], [psr[2 * us]])
                    for k in range(8):
                        mm(ul, wul[ws][:, k, f4 * 128:(f4 + 1) * 128], h2T[:, k, s0 + 1:s0 + 1 + sz], k == 0, k == 7,
                           rh + [r_wl[ws]], [psr[2 * us + 1]])
                    cb, gb = cbuf[us][:, 0:sz], gbuf[us][:, 0:sz]
                    act(AF.Identity, cb, ua[:, 1:sz + 1], [psr[2 * us], r_pf], [r_cbuf[us]], scale=w1, bias=bb)
                    dve("scalar_tensor_tensor", [psr[2 * us], r_pf, r_cbuf[us]], [r_cbuf[us]], out=cb,
                        in0=ua[:, 0:sz], scalar=w0, in1=cb, op0=ALU.mult, op1=ALU.add)
                    dve("scalar_tensor_tensor", [psr[2 * us], r_pf, r_cbuf[us]], [r_cbuf[us]], out=cb,
                        in0=ua[:, 2:sz + 2], scalar=w2, in1=cb, op0=ALU.mult, op1=ALU.add)
                    act(AF.Gelu_apprx_tanh, gb, cb, [r_cbuf[us]], [r_gbuf[us]])
                    dve("tensor_tensor", [r_gbuf[us], psr[2 * us + 1]], [r_actT[ws]],
                        out=actT[ws][:, f4, o0:o1], in0=gb, in1=ul, op=ALU.mult)

        def emit_down(gi):
            tcn, G = groups[gi]
            ws = gi % 2
            for t in range(8):
                for c in range(2):
                    pb = 5 + pycnt[0] % 2
                    pycnt[0] += 1
                    cs = slice(c * 512, (c + 1) * 512)
                    for f4 in range(4):
                        mm(ps[pb], actT[ws][:, f4, t * 128:(t + 1) * 128], wd[ws][:, f4, cs], f4 == 0, f4 == 3,
                           [r_actT[ws], r_wd[ws]], [psr[pb]])
                    dve("tensor_tensor", [psr[pb], r_xacc[t]], [r_xacc[t]], out=xacc[:, t, cs], in0=ps[pb],
                        in1=xacc[:, t, cs], op=ALU.add)

        def emit_xload(tcn):
            dma("sp", "xacc", xacc, xs_d[tcn * 1024:(tcn + 1) * 1024, :].rearrange("(t p) d -> p t d", p=128),
                [r_xs[8 * tcn + t] for t in range(8)], r_xacc)

        def emit_xstore(tcn):
            if not last:
                dma("sp", "xaccst", xs_d[tcn * 1024:(tcn + 1) * 1024, :].rearrange("(t p) d -> p t d", p=128), xacc,
                    r_xacc, [r_xs[8 * tcn + t] for t in range(8)])
                return
            for t in range(8):
                sl = t % 2
                ssx, tmpx, rsx = stat[:, 4 * sl:4 * sl + 1], stat[:, 4 * sl + 1:4 * sl + 2], stat[:, 4 * sl + 2:4 * sl + 3]
                act(AF.Square, cbuf[sl].bitcast(BF16), xacc[:, t, :], [r_xacc[t]], [r_cbuf[sl], r_st[sl]], accum_out=ssx)
                rstd_from_ss(ssx, rsx, D, r_st[sl], r_st[sl], tmpx, r_st[sl])
                dve("scalar_tensor_tensor", [r_xacc[t], r_st[sl], r_gA], [r_xacc[t]], out=xacc[:, t, :], in0=xacc[:, t, :],
                    scalar=rsx, in1=gA, op0=ALU.mult, op1=ALU.mult)
            out_ops.append(dma("sp", "ystore", y_d[tcn * 1024:(tcn + 1) * 1024, :].rearrange("(t p) d -> p t d", p=128),
                               xacc, r_xacc, []))

        emit_wload(0)
        emit_xload(0)
        emit_up(0)
        for gi in range(len(groups)):
            tcn, G = groups[gi]
            if gi + 1 < len(groups):
                emit_wload(gi + 1)
                emit_up(gi + 1)
            emit_down(gi)
            if G == 5:
                emit_xstore(tcn)
                if tcn < 3 and gi + 1 < len(groups):
                    emit_xload(tcn + 1)

    P.emit("sp", lambda e: e.nop(), deps=out_ops)
    P.finalize()
    lanes = list(P.lanes.keys())
    semctx = [nc.semaphore("s_" + ln) for ln in lanes]
    sems = {ln: c.__enter__() for ln, c in zip(lanes, semctx)}
    with nc.Block() as block:
        @block.sync
        def _(e):
            P.replay("sp", e, sems)

        @block.gpsimd
        def _(e):
            P.replay("pool", e, sems)

        @block.scalar
        def _(e):
            P.replay("act", e, sems)

        @block.vector
        def _(e):
            P.replay("dve", e, sems)

        @block.tensor
        def _(e):
            P.replay("pe", e, sems)
    for c in reversed(semctx):
        c.__exit__(None, None, None)
    psctx.__exit__(None, None, None)
    ctx.__exit__(None, None, None)
    nops = {k: len(v) for k, v in P.ops.items()}
    return nc, nops


_CACHE = {}


def kernel(**inputs):
    depth = DEPTH
    shared = _prep_shared(inputs, depth)
    shared["idn"] = np.eye(128, dtype=np.float32)
    x = np.asarray(inputs["x"], np.float32)
    nb = x.shape[0]
    if "nc" not in _CACHE:
        _CACHE["nc"] = build(depth)[0]
    nc = _CACHE["nc"]
    in_maps = []
    for b in range(nb):
        m = dict(shared)
        m["x"] = np.ascontiguousarray(x[b])
        in_maps.append(m)
    res = run_bass_kernel_spmd(nc, in_maps, core_ids=list(range(nb)))
    return np.stack([np.asarray(r["y"], np.float32) for r in res.results], axis=0)
```

```python
import numpy as np
import concourse.bass as bass
import concourse.mybir as mybir
from concourse.bass_utils import run_bass_kernel_spmd

F32 = mybir.dt.float32
BF16 = mybir.dt.bfloat16
AF = mybir.ActivationFunctionType
ALU = mybir.AluOpType

D = 1024
S = 4096
NT = S // 128
DEPTH = 4
D_ATT = 512
D_LRU = 512
D_IN = 2560
D_FF = 3072
NH = 8
EPS = 1e-6
NEG = -30000.0
NPF = 144
NBLK = 9


class Res:
    __slots__ = ("name", "w", "r")

    def __init__(self, name=""):
        self.name = name
        self.w = None
        self.r = {}


class Op:
    __slots__ = ("eng", "lane", "fn", "deps", "sig", "cnt", "inc", "idx")

    def __init__(self, eng, lane, fn, inc):
        self.idx = 0
        self.eng = eng
        self.lane = lane
        self.fn = fn
        self.deps = []
        self.sig = False
        self.cnt = 0
        self.inc = inc


class Prog:
    ENGS = ("pe", "act", "dve", "pool", "sp")

    def __init__(self):
        self.ops = {e: [] for e in self.ENGS}
        self.lanes = {}
        self.all_ops = []

    def emit(self, eng, fn, reads=(), writes=(), lane=None, deps=()):
        is_dma = lane is not None
        lane = lane if is_dma else eng
        op = Op(eng, lane, fn, 16 if is_dma else 1)
        op.sig = is_dma
        ds = set()
        for r in reads:
            if r.w is not None:
                ds.add(r.w)
        for w in writes:
            if w.w is not None:
                ds.add(w.w)
            for o in w.r.values():
                ds.add(o)
        for d in deps:
            if d is not None:
                ds.add(d)
        ds.discard(op)
        for d in ds:
            if d.lane == "pe" and op.lane == "pe":
                continue
            op.deps.append(d)
            d.sig = True
        for r in reads:
            r.r[lane] = op
        for w in writes:
            w.w = op
            w.r = {}
        op.idx = len(self.all_ops)
        self.ops[eng].append(op)
        self.lanes.setdefault(lane, []).append(op)
        self.all_ops.append(op)
        return op

    def finalize(self):
        for lane, ops in self.lanes.items():
            c = 0
            for op in ops:
                if op.sig:
                    c += op.inc
                    op.cnt = c

    def replay(self, eng, e, sems):
        waited = {}
        for op in self.ops[eng]:
            need = {}
            for d in op.deps:
                if d.cnt > need.get(d.lane, 0):
                    need[d.lane] = d.cnt
            for lane, c in need.items():
                if waited.get(lane, 0) < c:
                    e.wait_ge(sems[lane], c)
                    waited[lane] = c
            ins = op.fn(e)
            if op.sig:
                ins.then_inc(sems[op.lane], op.inc)


def _mb_index():
    GW, WH, WW = 64, 8, 16
    p = np.arange(128)
    ka, kc = p // 64, p % 64
    qb, qc = p // 64, p % 64
    cs = np.clip(qc - WW // 2, 0, GW - WW)
    colok = (kc[:, None] >= cs[None, :]) & (kc[:, None] < cs[None, :] + WW)
    dc = kc[:, None] - qc[None, :] + (WW - 1)
    blocks = []
    specs = [(-2, True), (-1, True), (0, True), (1, True), (2, True),
             (2, False), (3, False), (-2, False), (-3, False)]
    for dt, banded in specs:
        dr_rel = 2 * dt + ka[:, None] - qb[None, :]
        if banded:
            rowok = (dr_rel >= -4) & (dr_rel <= 3)
        else:
            rowok = np.ones_like(dr_rel, bool)
        ok = rowok & colok
        dr = dr_rel + (WH - 1)
        ok &= (dr >= 0) & (dr <= 2 * WH - 2) & (dc >= 0) & (dc <= 2 * WW - 2)
        blocks.append((np.clip(dr, 0, 14), np.clip(dc, 0, 30), ok))
    return blocks


def _prep_shared(inp, depth):
    f32 = np.float32
    blocks = _mb_index()
    rb = np.asarray(inp["rel_bias"], f32)
    mb = np.empty((depth, 128, NH, NBLK, 128), f32)
    for b, (dr, dc, ok) in enumerate(blocks):
        g = rb[:depth][:, :, dr, dc]
        g = np.where(ok[None, None], g, f32(NEG))
        mb[:, :, :, b, :] = g.transpose(0, 2, 1, 3)
    lw = np.asarray(inp["lru_w"], f32)[:depth]
    bd = np.zeros((depth, 128, 2, 2, 4, 128), f32)
    for cc in range(4):
        for hb in range(2):
            bd[:, hb * 64:(hb + 1) * 64, :, :, cc, hb * 64:(hb + 1) * 64] = \
                lw[:, :, :, 2 * cc + hb].transpose(0, 3, 1, 2, 4)
    bd = bd.reshape(depth, 128, 16 * 128)
    pf = np.zeros((depth, 128, NPF), f32)

    def fm(a, nch):
        return a.reshape(a.shape[:-1] + (nch, 128))

    for l in range(depth):
        cw = fm(np.asarray(inp["conv_lru_w"], f32)[l], 4)
        pf[l, :, 0:16] = cw.transpose(2, 1, 0).reshape(128, 16)
        pf[l, :, 16:20] = fm(np.asarray(inp["conv_lru_b"], f32)[l], 4).T
        lb = fm(np.asarray(inp["lru_b"], f32)[l], 4)
        pf[l, :, 20:36] = lb.transpose(3, 0, 1, 2).reshape(128, 16)
        ll = fm(np.asarray(inp["lru_lam"], f32)[l], 4)
        pf[l, :, 36:44] = ll.transpose(2, 0, 1).reshape(128, 8)
        pf[l, :, 44:48] = fm(np.asarray(inp["g_rec"], f32)[l], 4).T
        fw = fm(np.asarray(inp["conv_ffn_w"], f32)[l], 24)
        pf[l, :, 48:120] = fw.transpose(2, 1, 0).reshape(128, 72)
        pf[l, :, 120:144] = fm(np.asarray(inp["conv_ffn_b"], f32)[l], 24).T
    shared = {
        "w_in": np.ascontiguousarray(np.asarray(inp["w_in"], f32)[:depth]),
        "w_out": np.ascontiguousarray(np.asarray(inp["w_out"], f32)[:depth]),
        "w_up": np.ascontiguousarray(np.asarray(inp["w_up"], f32)[:depth]),
        "w_down": np.ascontiguousarray(np.asarray(inp["w_down"], f32)[:depth]),
        "g_mix": np.ascontiguousarray(np.asarray(inp["g_mix"], f32)[:depth]),
        "g_ffn": np.ascontiguousarray(np.asarray(inp["g_ffn"], f32)[:depth]),
        "g_att": np.ascontiguousarray(np.asarray(inp["g_att"], f32)[:depth]),
        "g_final": np.ascontiguousarray(np.asarray(inp["g_final"], f32)).reshape(1, D),
        "mb": mb.reshape(depth, 128, NH * NBLK * 128),
        "bd": bd,
        "pf": pf,
    }
    return shared


KW = 256
ARENA_KIB = 204


def _mk(fn, *a, **k):
    return lambda e: fn(e, *a, **k)


def build(depth=DEPTH, dbg=False):
    nc = bass.Bass("TRN2", target_bir_lowering=False)
    dt = nc.dram_tensor
    x_d = dt("x", [S, D], F32, kind="ExternalInput").ap()
    w_in_d = dt("w_in", [depth, D, D_IN], F32, kind="ExternalInput").ap()
    w_out_d = dt("w_out", [depth, D, D], F32, kind="ExternalInput").ap()
    w_up_d = dt("w_up", [depth, D, 2 * D_FF], F32, kind="ExternalInput").ap()
    w_down_d = dt("w_down", [depth, D_FF, D], F32, kind="ExternalInput").ap()
    g_mix_d = dt("g_mix", [depth, D], F32, kind="ExternalInput").ap()
    g_ffn_d = dt("g_ffn", [depth, D], F32, kind="ExternalInput").ap()
    g_att_d = dt("g_att", [depth, D_ATT], F32, kind="ExternalInput").ap()
    g_fin_d = dt("g_final", [1, D], F32, kind="ExternalInput").ap()
    mb_d = dt("mb", [depth, 128, NH * NBLK * 128], F32, kind="ExternalInput").ap()
    bd_d = dt("bd", [depth, 128, 16 * 128], F32, kind="ExternalInput").ap()
    pf_d = dt("pf", [depth, 128, NPF], F32, kind="ExternalInput").ap()
    idn_d = dt("idn", [128, 128], F32, kind="ExternalInput").ap()
    y_d = dt("y", [S, D], F32, kind="ExternalOutput").ap()
    xs_d = dt("xs", [S, D], F32, kind="Internal").ap()
    zr_d = dt("zr", [1024, S], F32, kind="Internal").ap()
    dbg_t = {}
    if dbg:
        dbg_t["hT"] = dt("d_hT", [128, 8 * S], BF16, kind="ExternalOutput").ap()
        dbg_t["qT"] = dt("d_qT", [128, 4 * S], BF16, kind="ExternalOutput").ap()
        dbg_t["kT"] = dt("d_kT", [128, 4 * S], BF16, kind="ExternalOutput").ap()
        dbg_t["V"] = dt("d_V", [128, NT * NH * 65], BF16, kind="ExternalOutput").ap()
        dbg_t["attT"] = dt("d_attT", [128, 4 * S], BF16, kind="ExternalOutput").ap()
        dbg_t["recT"] = dt("d_recT", [128, 4 * S], BF16, kind="ExternalOutput").ap()
        dbg_t["ssa"] = dt("d_ssa", [128, 64], F32, kind="ExternalOutput").ap()
        dbg_t["zr"] = dt("d_zr", [1024, S], F32, kind="ExternalOutput").ap()
        dbg_t["xD"] = dt("d_xD", [S, D], F32, kind="ExternalOutput").ap()

    P = Prog()
    ctx = nc.sbuf_tensor("arena", [128, ARENA_KIB * KW], F32)
    arena = ctx.__enter__()
    psctx = nc.psum_tensor("psall", [128, 4096], F32)
    ps_all = psctx.__enter__()
    ps = [ps_all[:, b * 512:(b + 1) * 512] for b in range(8)]
    psr = [Res("ps%d" % b) for b in range(8)]

    def reg(off_w, n_w, dtype=F32):
        a = arena[:, off_w:off_w + n_w]
        return a.bitcast(BF16) if dtype == BF16 else a

    def kib(k):
        return int(round(k * KW))

    ident = reg(0, 64, BF16)
    ones = reg(64, 1)
    pf_all = reg(72, depth * NPF).rearrange("p (l f) -> p l f", l=depth)
    drv = reg(648, 64)
    nch = drv[:, 0:8]
    bh = drv[:, 8:24]
    cwh = drv[:, 24:44]
    nch2 = drv[:, 44:52]
    ser = reg(712, 64)
    stat = reg(776, 64)
    ss_att = reg(840, 32)
    rstd_att = reg(872, 32)
    rstd_rec = reg(904, 32)
    gA = reg(1024, 1024)
    gB = reg(2048, 512)
    r_ident, r_ones, r_pf, r_drv, r_gA, r_gB = (Res(n) for n in ("ident", "ones", "pf", "drv", "gA", "gB"))
    r_ssatt, r_rstda, r_rstdr = Res("ssatt"), Res("rstda"), Res("rstdr")

    O_Q, O_K, O_V, O_X1, O_AT = kib(10), kib(42), kib(74), kib(106.5), kib(170.5)
    qT = reg(O_Q, kib(32), BF16).rearrange("p (c t) -> p c t", c=4)
    kT = reg(O_K, kib(32), BF16).rearrange("p (c t) -> p c t", c=4)
    Vt = reg(O_V, NT * NH * 65 // 2, BF16).rearrange("p (t h d) -> p t h d", t=NT, h=NH)
    hT = reg(O_X1, kib(64), BF16).rearrange("p (k t) -> p k t", k=8)
    attT = reg(O_AT, kib(32), BF16).rearrange("p (c t) -> p c t", c=4)
    recT = reg(O_Q, kib(32), BF16).rearrange("p (c t) -> p c t", c=4)

    def fence(res_list):
        f = {}
        for r in res_list:
            for o in ([r.w] if r.w is not None else []) + list(r.r.values()):
                if o.lane not in f or o.idx > f[o.lane].idx:
                    f[o.lane] = o
        return f

    def fenced(name, f):
        r = Res(name)
        r.r = dict(f)
        return r

    users = {rg: [] for rg in ("Q", "K", "V", "X1", "AT")}

    def phase_begin(regs):
        f = {}
        for rg in regs:
            for ln_, o in fence(users[rg]).items():
                if ln_ not in f or o.idx > f[ln_].idx:
                    f[ln_] = o
            users[rg] = []
        return f

    def newres(name, f, regs):
        r = fenced(name, f)
        for rg in regs:
            users[rg].append(r)
        return r

    lane_names = []

    def lane(name):
        if name not in lane_names:
            lane_names.append(name)
        return name

    def act(fn_, out, in_, reads, writes, **kw):
        return P.emit("act", lambda e: e.activation(out=out, in_=in_, func=fn_, **kw), reads, writes)

    def dve(method, reads, writes, **kw):
        return P.emit("dve", lambda e: getattr(e, method)(**kw), reads, writes)

    def pool(method, reads, writes, **kw):
        return P.emit("pool", lambda e: getattr(e, method)(**kw), reads, writes)

    def mm(out, lhsT, rhs, start, stop, reads, writes):
        return P.emit("pe", lambda e: e.matmul(out, lhsT, rhs, start=start, stop=stop), reads, writes)

    def tr(out, in_, reads, writes):
        return P.emit("pe", lambda e: e.transpose(out=out, in_=in_, identity=ident), reads + [r_ident], writes)

    last_pool_dma = [None]

    def dma(eng, ln, out, in_, reads, writes):
        if eng == "pool":
            op = P.emit(eng, lambda e: e.dma_start(out=out, in_=in_), reads, writes, lane=lane(ln),
                        deps=[last_pool_dma[0]])
            last_pool_dma[0] = op
            return op
        return P.emit(eng, lambda e: e.dma_start(out=out, in_=in_), reads, writes, lane=lane(ln))

    def rstd_from_ss(ss_ap, out_ap, n, r_in, r_out, tmp_ap, r_tmp):
        act(AF.Ln, tmp_ap, ss_ap, [r_in], [r_tmp], scale=1.0 / n, bias=EPS)
        act(AF.Exp, out_ap, tmp_ap, [r_tmp], [r_out], scale=-0.5)

    dma("pool", "setup_p", ident, idn_d, [], [r_ident])
    dve("memset", [], [r_ones], ap=ones, constant=1.0)
    dma("sp", "setup", pf_all, pf_d.rearrange("l p f -> p l f"), [], [r_pf])

    r_xs = [Res("xs%d" % i) for i in range(NT)]
    r_zr = [Res("zr%d" % i) for i in range(8)]
    x_tiles = x_d.rearrange("(t p) d -> t p d", p=128)
    xs_tiles = xs_d.rearrange("(t p) d -> t p d", p=128)
    y_tiles = y_d.rearrange("(t p) d -> t p d", p=128)
    out_ops = []

    for l in range(depth):
        src_tiles = x_tiles if l == 0 else xs_tiles
        last = (l == depth - 1)
        dma("sp", "gA", gA, g_mix_d[l:l + 1, :].partition_broadcast(128), [], [r_gA])
        dma("sp", "gB", gB, g_att_d[l:l + 1, :].partition_broadcast(128), [], [r_gB])
        lam = pf_all[:, l, 36:44]
        s_ax, s_y, s_t, s_z, s_z2, s_p, s_m = (ser[:, 8 * i:8 * i + 8] for i in range(7))
        r_s = [Res("ser%d" % i) for i in range(7)]
        act(AF.Abs, s_ax, lam, [r_pf], [r_s[0]])
        act(AF.Exp, s_y, s_ax, [r_s[0]], [r_s[1]], scale=-1.0)
        dve("tensor_scalar", [r_s[1]], [r_s[2]], out=s_t, in0=s_y, scalar1=2.0, scalar2=None, op0=ALU.add)
        dve("reciprocal", [r_s[2]], [r_s[2]], out=s_t, in_=s_t)
        dve("tensor_tensor", [r_s[1], r_s[2]], [r_s[3]], out=s_z, in0=s_y, in1=s_t, op=ALU.mult)
        dve("tensor_tensor", [r_s[3]], [r_s[4]], out=s_z2, in0=s_z, in1=s_z, op=ALU.mult)
        dve("tensor_scalar", [r_s[4]], [r_s[5]], out=s_p, in0=s_z2, scalar1=1.0 / 9, scalar2=1.0 / 7,
            op0=ALU.mult, op1=ALU.add)
        for cst in (1.0 / 5, 1.0 / 3, 1.0):
            dve("tensor_tensor", [r_s[5], r_s[4]], [r_s[5]], out=s_p, in0=s_p, in1=s_z2, op=ALU.mult)
            dve("tensor_scalar", [r_s[5]], [r_s[5]], out=s_p, in0=s_p, scalar1=cst, scalar2=None, op0=ALU.add)
        dve("scalar_tensor_tensor", [r_s[3], r_s[5]], [r_s[5]], out=s_p, in0=s_z, scalar=2.0, in1=s_p,
            op0=ALU.mult, op1=ALU.mult)
        act(AF.Relu, s_m, lam, [r_pf], [r_s[6]], scale=-1.0)
        dve("tensor_tensor", [r_s[5], r_s[6]], [r_s[6]], out=s_m, in0=s_m, in1=s_p, op=ALU.add)
        dve("tensor_scalar", [r_s[6]], [r_drv], out=nch, in0=s_m, scalar1=-4.0, scalar2=None, op0=ALU.mult)
        dve("tensor_scalar", [r_s[6]], [r_drv], out=nch2, in0=s_m, scalar1=-8.0, scalar2=None, op0=ALU.mult)
        dve("tensor_scalar", [r_pf], [r_drv], out=bh, in0=pf_all[:, l, 20:36], scalar1=0.5, scalar2=None,
            op0=ALU.mult)
        dve("tensor_scalar", [r_pf], [r_drv], out=cwh, in0=pf_all[:, l, 0:20], scalar1=0.5, scalar2=None,
            op0=ALU.mult)

        O_W = O_AT
        xt = [reg(O_W + kib(4) * i, kib(4)) for i in range(4)]
        ht = [reg(O_W + kib(16) + kib(2) * i, kib(2), BF16) for i in range(3)]
        junk = reg(O_W + kib(22), kib(2), BF16)
        tmpP = [reg(O_W + kib(24) + kib(4) * i, kib(4)) for i in range(2)]
        fA = phase_begin(["Q", "K", "V", "X1", "AT"])
        r_xt = [newres("xt%d" % i, fA, ["AT"]) for i in range(4)]
        r_ht = [newres("ht%d" % i, fA, ["AT"]) for i in range(3)]
        r_junk = newres("junk", fA, ["AT"])
        r_tmpP = [newres("tmpP%d" % i, fA, ["AT"]) for i in range(2)]
        r_st = [Res("st%d" % i) for i in range(4)]
        r_hT = [newres("hT%d" % i, fA, ["X1"]) for i in range(NT)]
        def A0_S0(i):
            sl = i % 4
            dma("sp", "xt%d" % sl, xt[sl], src_tiles[i], [r_xs[i]], [r_xt[sl]])

        def A0_S1(i):
            sl, s2 = i % 4, i % 2
            ssx, tmpx, rsx = stat[:, 4 * s2:4 * s2 + 1], stat[:, 4 * s2 + 1:4 * s2 + 2], stat[:, 4 * s2 + 2:4 * s2 + 3]
            act(AF.Square, junk, xt[sl], [r_xt[sl]], [r_junk, r_st[s2]], accum_out=ssx)
            rstd_from_ss(ssx, rsx, D, r_st[s2], r_st[s2], tmpx, r_st[s2])

        def A0_S2(i):
            sl, hl, s2 = i % 4, i % 3, i % 2
            rsx = stat[:, 4 * s2 + 2:4 * s2 + 3]
            dve("scalar_tensor_tensor", [r_xt[sl], r_st[s2], r_gA], [r_ht[hl]], out=ht[hl], in0=xt[sl],
                scalar=rsx, in1=gA, op0=ALU.mult, op1=ALU.mult)

        def A0_S3(i):
            hl, pb = i % 3, i % 2
            pst = ps[pb].bitcast(BF16)
            for k in range(8):
                tr(pst[:, k * 128:(k + 1) * 128], ht[hl][:, k * 128:(k + 1) * 128], [r_ht[hl]], [psr[pb]])

        def A0_S4(i):
            pb = i % 2
            pst = ps[pb].bitcast(BF16)
            o = hT[:, :, i * 128:(i + 1) * 128]
            s_ = pst.rearrange("p (k t) -> p k t", k=8)
            if i % 2 == 0:
                act(AF.Copy, o, s_, [psr[pb]], [r_hT[i]])
            else:
                dve("tensor_copy", [psr[pb]], [r_hT[i]], out=o, in_=s_)

        for i_ in range(3):
            A0_S0(i_)
        for t_ in range(NT + 3):
            if t_ < NT:
                A0_S1(t_)
            if 0 <= t_ - 1 < NT:
                A0_S2(t_ - 1)
            if t_ + 3 < NT:
                A0_S0(t_ + 3)
            if 0 <= t_ - 2 < NT:
                A0_S3(t_ - 2)
            if 0 <= t_ - 3 < NT:
                A0_S4(t_ - 3)
        if dbg and l == 0:
            out_ops.append(dma("sp", "dbgx1", dbg_t["hT"], hT.rearrange("p k t -> p (k t)"), r_hT, []))

        f_w = fence(r_xt + r_ht + [r_junk] + r_tmpP)
        wb = [reg(O_W + kib(8) * i, kib(8), BF16).rearrange("p (k c) -> p k c", k=8) for i in range(2)]
        r_wb = [newres("wb%d" % i, f_w, ["AT"]) for i in range(2)]
        stg = [reg(O_W + kib(16) + kib(2) * i, kib(2)) for i in range(4)]
        r_stg = [newres("stg%d" % i, f_w, ["AT"]) for i in range(4)]
        r_qT = [newres("qT%d" % i, fA, ["Q"]) for i in range(8)]
        r_kT = [newres("kT%d" % i, fA, ["K"]) for i in range(8)]
        r_V = [newres("V%d" % i, fA, ["V"]) for i in range(NT)]
        r_Vones = newres("Vones", fA, ["V"])
        pool("memset", [], [r_Vones], ap=Vt[:, :, :, 64:65], constant=1.0)
        w_l = w_in_d[l].rearrange("(k p) c -> p k c", p=128)
        nev = 0
        bank = 0
        stg_i = 0
        for g in range(5):
            sl = g % 2
            dma("pool", "wb%d" % sl, wb[sl], w_l[:, :, g * 512:(g + 1) * 512], [], [r_wb[sl]])
            if g == 2:
                for i in range(NT):
                    b = bank % 8
                    bank += 1
                    for k in range(8):
                        mm(ps[b], hT[:, k, i * 128:(i + 1) * 128], wb[sl][:, k, :], k == 0, k == 7,
                           [r_hT[i], r_wb[sl]], [psr[b]])
                    o = Vt[:, i, :, 0:64]
                    s_ = ps[b].rearrange("p (h d) -> p h d", h=NH)
                    if nev % 2 == 0:
                        act(AF.Copy, o, s_, [psr[b]], [r_V[i]])
                    else:
                        dve("tensor_copy", [psr[b]], [r_V[i]], out=o, in_=s_)
                    nev += 1
                continue
            for c in range(4):
                for tc in range(8):
                    b = bank % 8
                    bank += 1
                    rd = [r_hT[4 * tc + j] for j in range(4)] + [r_wb[sl]]
                    for k in range(8):
                        mm(ps[b], wb[sl][:, k, c * 128:(c + 1) * 128], hT[:, k, tc * 512:(tc + 1) * 512],
                           k == 0, k == 7, rd, [psr[b]])
                    if g < 2:
                        dstT, rr = (qT, r_qT) if g == 0 else (kT, r_kT)
                        o = dstT[:, c, tc * 512:(tc + 1) * 512]
                        sc = 0.125 if g == 0 else 1.0
                        if nev % 2 == 0:
                            act(AF.Copy, o, ps[b], [psr[b]], [rr[tc]], scale=sc)
                        else:
                            dve("tensor_scalar", [psr[b]], [rr[tc]], out=o, in0=ps[b], scalar1=sc, scalar2=None,
                                op0=ALU.mult)
                    else:
                        ss_ = stg_i % 4
                        stg_i += 1
                        if nev % 2 == 0:
                            act(AF.Copy, stg[ss_], ps[b], [psr[b]], [r_stg[ss_]])
                        else:
                            dve("tensor_copy", [psr[b]], [r_stg[ss_]], out=stg[ss_], in_=ps[b])
                        row = (g - 3) * 4 + c
                        dma("sp", "stg%d" % ss_, zr_d[row * 128:(row + 1) * 128, tc * 512:(tc + 1) * 512], stg[ss_],
                            [r_stg[ss_]], [r_zr[row]])
                    nev += 1
        if dbg and l == 0:
            out_ops.append(dma("sp", "dbgx2", dbg_t["qT"], qT.rearrange("p c t -> p (c t)"), r_qT, []))
            out_ops.append(dma("sp", "dbgx3", dbg_t["kT"], kT.rearrange("p c t -> p (c t)"), r_kT, []))
            out_ops.append(dma("sp", "dbgx4", dbg_t["V"], Vt.rearrange("p t h d -> p (t h d)"), r_V + [r_Vones], []))
            out_ops.append(dma("sp", "dbgx5", dbg_t["zr"], zr_d, r_zr, []))
        if dbg == "A":
            break

        fB = phase_begin(["X1", "AT"])
        mbt = reg(O_X1, kib(18), BF16).rearrange("p (h b q) -> p h b q", h=NH, b=NBLK)
        r_mb = newres("mb", fB, ["X1"])
        dma("pool", "mb", reg(O_X1, kib(18), BF16), mb_d[l], [], [r_mb])
        ob = O_X1 + kib(18)
        qz = [reg(ob + 512 * i, 512, BF16).rearrange("p (h q) -> p h q", h=NH) for i in range(2)]
        pTb = [reg(ob + 1024 + 320 * i, 320, BF16) for i in range(2)]
        attb = [reg(ob + 1920 + 512 * i, 512) for i in range(2)]
        attn = [reg(ob + 2944 + 256 * i, 256, BF16) for i in range(2)]
        junkb = reg(ob + 3456, 256, BF16)
        rsb = reg(ob + 3712, 16)
        r_qz = [newres("qz%d" % i, fB, ["X1"]) for i in range(2)]
        r_pTb = [newres("pTb%d" % i, fB, ["X1"]) for i in range(2)]
        r_attb = [newres("attb%d" % i, fB, ["X1"]) for i in range(2)]
        r_attn = [newres("attn%d" % i, fB, ["X1"]) for i in range(2)]
        r_junkb = newres("junkb", fB, ["X1"])
        r_rsb = [newres("rsb%d" % i, fB, ["X1"]) for i in range(2)]
        r_attT = [newres("attT%d" % i, fB, ["AT"]) for i in range(NT)]
        for i in range(2):
            pool("memset", [], [r_qz[i]], ap=qz[i], constant=0.0)

        def tile_plan(j):
            if j == 0:
                return [0, 1, 2, 3], [2, 3, 5, 6]
            if j == 1:
                return [0, 1, 2, 3], [1, 2, 3, 5]
            if j == NT - 2:
                return [28, 29, 30, 31], [7, 1, 2, 3]
            if j == NT - 1:
                return [28, 29, 30, 31], [8, 7, 1, 2]
            return [j - 2, j - 1, j, j + 1, j + 2], [0, 1, 2, 3, 4]

        def runs(bl):
            out, st = [], 0
            for i in range(1, len(bl) + 1):
                if i == len(bl) or bl[i] != bl[i - 1] + 1:
                    out.append((st, i))
                    st = i
            return out

        units = [(j, h) for j in range(NT) for h in range(NH)]

        def spv(s):
            return ps_all[:, s * 1024:s * 1024 + 1024].rearrange("p (n q) -> p n q", n=8)

        def r_sp(s):
            return [psr[2 * s], psr[2 * s + 1]]

        def emit_qz(j):
            qs = j % 2
            ts_ = slice(j * 128, (j + 1) * 128)
            pool("tensor_copy", [r_qT[j // 4]], [r_qz[qs]], out=qz[qs][0:64, 0::2, :], in_=qT[0:64, :, ts_])
            pool("tensor_copy", [r_qT[j // 4]], [r_qz[qs]], out=qz[qs][64:128, 1::2, :], in_=qT[64:128, :, ts_])

        def emit_qk(u):
            j, h = units[u]
            tiles, blks = tile_plan(j)
            s = u % 2
            c = h // 2
            if u == 0:
                emit_qz(0)
            if h == 3 and j + 1 < NT:
                emit_qz(j + 1)
            for n, (t, b_) in enumerate(zip(tiles, blks)):
                pr_ = [psr[2 * s + n // 4]]
                mm(spv(s)[:, n, :], kT[:, c, t * 128:(t + 1) * 128], qz[j % 2][:, h, :], True, False,
                   [r_kT[t // 4], r_qz[j % 2]], pr_)
                mm(spv(s)[:, n, :], ident, mbt[:, h, b_, :], False, True, [r_ident, r_mb], pr_)

        def emit_exp(u):
            j, h = units[u]
            tiles, blks = tile_plan(j)
            nb = len(tiles)
            s = u % 2
            act(AF.Exp, pTb[s][:, 0:nb * 128], ps_all[:, s * 1024:s * 1024 + nb * 128],
                [psr[2 * s], psr[2 * s + 1]], [r_pTb[s]])

        def emit_pv(u):
            j, h = units[u]
            tiles, blks = tile_plan(j)
            nb = len(tiles)
            s = u % 2
            ob_ = 4 + h // 4
            for n, t in enumerate(tiles):
                mm(ps[ob_][:, (h % 4) * 128:(h % 4) * 128 + 65], pTb[s][:, n * 128:(n + 1) * 128], Vt[:, t, h, :],
                   n == 0, n == nb - 1, [r_pTb[s], r_V[t], r_Vones], [psr[ob_]])
            if h % 4 == 3:
                hf = h // 4
                sl = j % 2
                opv = ps[ob_].rearrange("p (g d) -> p g d", g=4)
                dve("reciprocal", [psr[ob_]], [r_rsb[hf]], out=rsb[:, 4 * hf:4 * hf + 4], in_=opv[:, :, 64])
                dve("tensor_tensor", [psr[ob_], r_rsb[hf]], [r_attb[sl]],
                    out=attb[sl][:, hf * 256:(hf + 1) * 256].rearrange("p (g d) -> p g d", g=4),
                    in0=opv[:, :, 0:64], in1=rsb[:, 4 * hf:4 * hf + 4].unsqueeze(2).to_broadcast([128, 4, 64]),
                    op=ALU.mult)

        def emit_post(j):
            sl = j % 2
            act(AF.Square, junkb, attb[sl], [r_attb[sl]], [r_junkb, r_ssatt], accum_out=ss_att[:, j:j + 1])
            dve("tensor_tensor", [r_attb[sl], r_gB], [r_attn[sl]], out=attn[sl], in0=attb[sl], in1=gB, op=ALU.mult)

        def emit_tr(j):
            sl = j % 2
            pb = 6 + j % 2
            pst = ps[pb].bitcast(BF16)
            for k in range(4):
                tr(pst[:, k * 128:(k + 1) * 128], attn[sl][:, k * 128:(k + 1) * 128], [r_attn[sl]], [psr[pb]])
            act(AF.Copy, attT[:, :, j * 128:(j + 1) * 128], pst[:, 0:512].rearrange("p (k t) -> p k t", k=4),
                [psr[pb]], [r_attT[j]])

        def after_pv(u):
            j, h = units[u]
            if h == NH - 1:
                emit_post(j)
            if h == 1 and j > 0:
                emit_tr(j - 1)

        emit_qk(0)
        for u in range(len(units)):
            if u + 1 < len(units):
                emit_qk(u + 1)
            emit_exp(u)
            if u >= 1:
                emit_pv(u - 1)
                after_pv(u - 1)
        emit_pv(len(units) - 1)
        after_pv(len(units) - 1)
        emit_tr(NT - 1)
        if dbg and l == 0:
            out_ops.append(dma("sp", "dbgy1", dbg_t["attT"], attT.rearrange("p c t -> p (c t)"), r_attT, []))
            out_ops.append(dma("sp", "dbgy2", dbg_t["ssa"][:, 0:32], ss_att, [r_ssatt], []))
        if dbg == "B":
            break

        fC = phase_begin(["Q", "K", "V", "X1"])
        o = O_K
        xr = reg(o, 4104); o += 4104
        xch = reg(o, 4096); o += 4096
        xcb = reg(o, 2048, BF16); o += 2048
        Hf = reg(o, 4096); o += 4096
        Ab = reg(o, 4096); o += 4096
        Ib = reg(o, 4096); o += 4096
        Tb = reg(o, 4096); o += 4096
        hbk = [reg(o + 512 * i, 512) for i in range(2)]; o += 1024
        sqb = [reg(o + 512 * i, 512) for i in range(2)]; o += 1024
        bdt = reg(o, 1024, BF16).rearrange("p (m c) -> p m c", m=16); o += 1024
        assert o <= O_AT
        gate = xr[:, 2:4098]
        LR = ["K", "V", "X1"]
        r_xr = newres("xr", fC, LR)
        r_xrpad = newres("xrpad", fC, LR)
        r_xch = [newres("xch%d" % i, fC, LR) for i in range(2)]
        r_xcb = [newres("xcb%d" % i, fC, LR) for i in range(2)]
        r_Hf = [newres("Hf%d" % i, fC, LR) for i in range(8)]
        r_A = [newres("A%d" % i, fC, LR) for i in range(8)]
        r_I = [newres("I%d" % i, fC, LR) for i in range(8)]
        r_T = [newres("T%d" % i, fC, LR) for i in range(2)]
        r_hbk = [newres("hbk%d" % i, fC, LR) for i in range(2)]
        r_sqb = [newres("sqb%d" % i, fC, LR) for i in range(2)]
        r_bd = newres("bd", fC, LR)
        r_recT = [newres("recT%d" % i, fC, ["Q"]) for i in range(8)]
        dma("pool", "bd", bdt.rearrange("p m c -> p (m c)"), bd_d[l], [], [r_bd])
        pool("memset", [], [r_xrpad], ap=xr[:, 0:2], constant=0.0)
        pool("memset", [], [r_xrpad], ap=xr[:, 4098:4104], constant=0.0)
        pfl = pf_all[:, l, :]
        gcnt = 0
        for cc in range(4):
            dma("sp", "xr", xr[:, 2:4098], zr_d[cc * 128:(cc + 1) * 128, :], [r_zr[cc]], [r_xr])
            for hf in range(2):
                c0 = hf * 2048
                act(AF.Identity, xch[:, c0:c0 + 2048], xr[:, 2 + c0:2 + c0 + 2048], [r_xr, r_xrpad, r_drv], [r_xch[hf]],
                    scale=cwh[:, cc * 4 + 2:cc * 4 + 3], bias=cwh[:, 16 + cc:17 + cc])
            for hf in range(2):
                c0 = hf * 2048
                dst = xch[:, c0:c0 + 2048]
                for tap in (0, 1, 3):
                    dve("scalar_tensor_tensor", [r_xr, r_xrpad, r_drv, r_xch[hf]], [r_xch[hf]], out=dst,
                        in0=xr[:, tap + c0:tap + c0 + 2048], scalar=cwh[:, cc * 4 + tap:cc * 4 + tap + 1], in1=dst,
                        op0=ALU.mult, op1=ALU.add)
            for hf in range(2):
                c0 = hf * 2048
                act(AF.Copy, xcb[:, c0:c0 + 2048], xch[:, c0:c0 + 2048], [r_xch[hf]], [r_xcb[hf]], scale=2.0)
            dma("sp", "xr", gate, zr_d[512 + cc * 128:512 + (cc + 1) * 128, :], [r_zr[4 + cc]], [r_xr])
            for d_ in range(2):
                order = list(range(8))
                for tc in order:
                    s = gcnt % 2
                    gcnt += 1
                    cs = slice(tc * 512, (tc + 1) * 512)
                    mm(ps[2 * s], bdt[:, d_ * 8 + cc, :], xcb[:, cs], True, True, [r_bd, r_xcb[tc // 4]], [psr[2 * s]])
                    mm(ps[2 * s + 1], bdt[:, d_ * 8 + 4 + cc, :], xcb[:, cs], True, True, [r_bd, r_xcb[tc // 4]],
                       [psr[2 * s + 1]])
                    act(AF.Tanh, Ab[:, cs], ps[2 * s], [psr[2 * s], r_drv], [r_A[tc]], scale=0.5,
                        bias=bh[:, d_ * 8 + cc:d_ * 8 + cc + 1])
                    act(AF.Tanh, Ib[:, cs], ps[2 * s + 1], [psr[2 * s + 1], r_drv], [r_I[tc]], scale=0.5,
                        bias=bh[:, d_ * 8 + 4 + cc:d_ * 8 + 5 + cc])
                ncol = nch[:, d_ * 4 + cc:d_ * 4 + cc + 1]
                ncol2 = nch2[:, d_ * 4 + cc:d_ * 4 + cc + 1]
                hord = [0, 1]
                for hf in hord:
                    hs = slice(hf * 2048, hf * 2048 + 2048)
                    rA = r_A[4 * hf:4 * hf + 4]
                    act(AF.Exp, Tb[:, hs], Ab[:, hs], rA + [r_drv], [r_T[hf]], scale=ncol2, bias=ncol2)
                    act(AF.Exp, Ab[:, hs], Ab[:, hs], rA + [r_drv], rA, scale=ncol, bias=ncol)
                    rI = r_I[4 * hf:4 * hf + 4]
                    dve("scalar_tensor_tensor", rI + [r_xch[hf]], rI, out=Ib[:, hs], in0=Ib[:, hs], scalar=1.0,
                        in1=xch[:, hs], op0=ALU.add, op1=ALU.mult)
                for hf in hord:
                    hs = slice(hf * 2048, hf * 2048 + 2048)
                    rA, rI = r_A[4 * hf:4 * hf + 4], r_I[4 * hf:4 * hf + 4]
                    act(AF.Sqrt, Tb[:, hs], Tb[:, hs], [r_T[hf]], [r_T[hf]], scale=-1.0, bias=1.0)
                    dve("tensor_tensor", rI + [r_T[hf]], rI, out=Ib[:, hs], in0=Ib[:, hs], in1=Tb[:, hs], op=ALU.mult)
                    if d_ == 0:
                        init = 0.0 if hf == 0 else Hf[:, 2047:2048]
                        dve("tensor_tensor_scan", rA + rI + (r_Hf[0:4] if hf == 1 else []), r_Hf[4 * hf:4 * hf + 4],
                            out=Hf[:, hs], data0=Ab[:, hs], data1=Ib[:, hs], initial=init, op0=ALU.mult, op1=ALU.add)
                if d_ == 0:
                    act(AF.Gelu_apprx_tanh, gate, gate, [r_xr], [r_xr])
                else:
                    for tc in range(7, -1, -1):
                        s = tc % 2
                        cs = slice(tc * 512, (tc + 1) * 512)
                        init = 0.0 if tc == 7 else hbk[1 - s][:, 0:1]
                        dve("tensor_tensor_scan", [r_A[tc], r_I[tc], r_hbk[1 - s]], [r_hbk[s]],
                            out=hbk[s][:, ::-1], data0=Ab[:, cs][:, ::-1], data1=Ib[:, cs][:, ::-1], initial=init,
                            op0=ALU.mult, op1=ALU.add)
                        dve("tensor_tensor", [r_Hf[tc], r_hbk[s]], [r_Hf[tc]], out=Hf[:, cs], in0=Hf[:, cs], in1=hbk[s],
                            op=ALU.add)
                        dve("tensor_tensor", [r_Hf[tc], r_xr], [r_Hf[tc]], out=Hf[:, cs], in0=Hf[:, cs], in1=gate[:, cs],
                            op=ALU.mult)
                        act(AF.Identity, recT[:, cc, cs], Hf[:, cs], [r_Hf[tc], r_pf], [r_recT[tc]],
                            scale=pfl[:, 44 + cc:45 + cc])
                        act(AF.Square, sqb[s], Hf[:, cs], [r_Hf[tc]], [r_sqb[s]])
                        for ti in range(4):
                            tok = tc * 4 + ti
                            col = cc * 32 + tok
                            mm(ps[4][:, col:col + 1], sqb[s][:, ti * 128:(ti + 1) * 128], ones, True, True,
                               [r_sqb[s], r_ones], [psr[4]])
        if dbg and l == 0:
            out_ops.append(dma("sp", "dbgy3", dbg_t["recT"], recT.rearrange("p c t -> p (c t)"), r_recT, []))
        if dbg == "C":
            break

        fD = phase_begin(["K", "V", "X1"])
        o = O_K
        h2T = reg(o, 16392, BF16).rearrange("p (k t) -> p k t", k=8); o += 16392
        wo = reg(o, 4096, BF16).rearrange("p (k c) -> p k c", k=8); o += 4096
        xt = [reg(o + 1024 * i, 1024) for i in range(4)]; o += 4096
        ht = [reg(o + 512 * i, 512, BF16) for i in range(3)]; o += 1536
        junk = reg(o, 512, BF16); o += 512
        tmpP = [reg(o + 1024 * i, 1024) for i in range(2)]; o += 2048
        assert o <= O_AT
        LD = ["K", "V", "X1"]
        r_h2T = [newres("h2T%d" % i, fD, LD) for i in range(NT)]
        r_h2pad = newres("h2pad", fD, LD)
        r_wo = newres("wo", fD, LD)
        r_xt = [newres("xtD%d" % i, fD, LD) for i in range(4)]
        r_ht = [newres("htD%d" % i, fD, LD) for i in range(3)]
        r_junk = newres("junkD", fD, LD)
        r_tmpP = [newres("tmpPD%d" % i, fD, LD) for i in range(2)]
        pool("memset", [], [r_h2pad], ap=h2T[:, :, 0:1], constant=0.0)
        pool("memset", [], [r_h2pad], ap=h2T[:, :, 4097:4098], constant=0.0)
        dma("pool", "wo", wo, w_out_d[l].rearrange("(k p) c -> p k c", p=128), [], [r_wo])
        dma("sp", "gA", gA, g_ffn_d[l:l + 1, :].partition_broadcast(128), [], [r_gA])
        tmpa = stat[:, 16:48]
        r_tmpa = Res("tmpa")
        rstd_from_ss(ss_att, rstd_att, D_ATT, r_ssatt, r_rstda, tmpa, r_tmpa)
        ssr = stat[:, 48:80] if False else reg(936, 32)
        r_ssr = Res("ssr")
        dve("tensor_copy", [psr[4]], [r_ssr], out=ssr, in_=ps[4][:, 0:32])
        for cc_ in range(1, 4):
            dve("tensor_tensor", [psr[4], r_ssr], [r_ssr], out=ssr, in0=ssr, in1=ps[4][:, cc_ * 32:cc_ * 32 + 32], op=ALU.add)
        rstd_from_ss(ssr, rstd_rec, D_LRU, r_ssr, r_rstdr, tmpa, r_tmpa)
        def D_S0(i):
            sl = i % 4
            dma("sp", "xtD%d" % sl, xt[sl], src_tiles[i], [r_xs[i]], [r_xt[sl]])

        def stage1D(i):
            sl, s2 = i % 4, i % 2
            for c in range(2):
                cs = slice(c * 512, (c + 1) * 512)
                for k in range(4):
                    mm(ps[2 * c], attT[:, k, i * 128:(i + 1) * 128], wo[:, k, cs], k == 0, k == 3,
                       [r_attT[i], r_wo], [psr[2 * c]])
                for k in range(4):
                    mm(ps[2 * c + 1], recT[:, k, i * 128:(i + 1) * 128], wo[:, 4 + k, cs], k == 0, k == 3,
                       [r_recT[i // 4], r_wo], [psr[2 * c + 1]])
                dve("scalar_tensor_tensor", [psr[2 * c], r_rstda, r_xt[sl]], [r_xt[sl]], out=xt[sl][:, cs],
                    in0=ps[2 * c], scalar=rstd_att[:, i:i + 1], in1=xt[sl][:, cs], op0=ALU.mult, op1=ALU.add)
                dve("scalar_tensor_tensor", [psr[2 * c + 1], r_rstdr, r_xt[sl]], [r_xt[sl]], out=xt[sl][:, cs],
                    in0=ps[2 * c + 1], scalar=rstd_rec[:, i:i + 1], in1=xt[sl][:, cs], op0=ALU.mult, op1=ALU.add)
            dma("sp", "xsD%d" % sl, xs_tiles[i], xt[sl], [r_xt[sl]], [r_xs[i]])
            ssx, tmpx, rsx = stat[:, 4 * s2:4 * s2 + 1], stat[:, 4 * s2 + 1:4 * s2 + 2], stat[:, 4 * s2 + 2:4 * s2 + 3]
            act(AF.Square, junk, xt[sl], [r_xt[sl]], [r_junk, r_st[s2]], accum_out=ssx)
            rstd_from_ss(ssx, rsx, D, r_st[s2], r_st[s2], tmpx, r_st[s2])

        def D_S2(i):
            sl, hl, s2 = i % 4, i % 3, i % 2
            rsx = stat[:, 4 * s2 + 2:4 * s2 + 3]
            dve("scalar_tensor_tensor", [r_xt[sl], r_st[s2], r_gA], [r_ht[hl]], out=ht[hl], in0=xt[sl],
                scalar=rsx, in1=gA, op0=ALU.mult, op1=ALU.mult)

        def D_S3(i):
            hl, pb = i % 3, 6 + i % 2
            pst = ps[pb].bitcast(BF16)
            for k in range(8):
                tr(pst[:, k * 128:(k + 1) * 128], ht[hl][:, k * 128:(k + 1) * 128], [r_ht[hl]], [psr[pb]])

        def D_S4(i):
            pb = 6 + i % 2
            pst = ps[pb].bitcast(BF16)
            o_ = h2T[:, :, 1 + i * 128:1 + (i + 1) * 128]
            s_ = pst.rearrange("p (k t) -> p k t", k=8)
            act(AF.Copy, o_, s_, [psr[pb]], [r_h2T[i]])

        for i_ in range(3):
            D_S0(i_)
        for t_ in range(NT + 3):
            if t_ < NT:
                stage1D(t_)
            if 0 <= t_ - 1 < NT:
                D_S2(t_ - 1)
            if t_ + 3 < NT:
                D_S0(t_ + 3)
            if 0 <= t_ - 2 < NT:
                D_S3(t_ - 2)
            if 0 <= t_ - 3 < NT:
                D_S4(t_ - 3)
        if dbg and l == 0:
            out_ops.append(dma("sp", "dbgy4", dbg_t["xD"], xs_d, r_xs, []))
        if dbg == "D":
            break

        fE = phase_begin(["X1", "AT", "Q"])
        LE = ["X1", "AT", "Q"]
        xacc = reg(O_X1, 8192).rearrange("p (t d) -> p t d", t=8)
        wbase = O_X1 + 8192
        wua = [reg(wbase + 6144 * i, 2048, BF16).rearrange("p (k c) -> p k c", k=8) for i in range(2)]
        wul = [reg(wbase + 6144 * i + 2048, 2048, BF16).rearrange("p (k c) -> p k c", k=8) for i in range(2)]
        wd = [reg(wbase + 6144 * i + 4096, 2048, BF16).rearrange("p (f c) -> p f c", f=4) for i in range(2)]
        abase = wbase + 12288
        actT = [reg(abase + 2048 * i, 2048, BF16).rearrange("p (f t) -> p f t", f=4) for i in range(2)]
        assert abase + 4096 <= ARENA_KIB * KW
        cbuf = [reg(O_Q + 512 * i, 512) for i in range(2)]
        gbuf = [reg(O_Q + 1024 + 512 * i, 512) for i in range(2)]
        stgE = [reg(O_Q + 2048 + 512 * i, 512) for i in range(2)]
        r_xacc = [newres("xacc%d" % i, fE, LE) for i in range(8)]
        r_wu = [newres("wu%d" % i, fE, LE) for i in range(2)]
        r_wl = [newres("wl%d" % i, fE, LE) for i in range(2)]
        r_wd = [newres("wd%d" % i, fE, LE) for i in range(2)]
        r_actT = [newres("actT%d" % i, fE, LE) for i in range(2)]
        r_cbuf = [newres("cbuf%d" % i, fE, LE) for i in range(2)]
        r_gbuf = [newres("gbuf%d" % i, fE, LE) for i in range(2)]
        r_stgE = [newres("stgE%d" % i, fE, LE) for i in range(2)]
        if last:
            dma("sp", "gA", gA, g_fin_d.partition_broadcast(128), [], [r_gA])
        wup = w_up_d[l].rearrange("(k p) c -> p k c", p=128)
        wdn = w_down_d[l].rearrange("(f p) c -> p f c", p=128)
        groups = [(tcn, G) for tcn in range(4) for G in range(6)]
        if dbg == "E1":
            groups = groups[:1]
        if dbg == "E2":
            groups = groups[:6]
        if dbg == "E3":
            groups = groups[:3]
        ucnt = [0]
        SUBO = [0, 342, 684, 1024]
        pycnt = [0]

        def emit_wload(gi):
            tcn, G = groups[gi]
            ws = gi % 2
            dma("pool", "wua%d" % ws, wua[ws], wup[:, :, G * 512:(G + 1) * 512], [], [r_wu[ws]])
            dma("pool", "wul%d" % ws, wul[ws], wup[:, :, D_FF + G * 512:D_FF + (G + 1) * 512], [], [r_wl[ws]])
            dma("pool", "wd%d" % ws, wd[ws], wdn[:, G * 4:(G + 1) * 4, :], [], [r_wd[ws]])

        def emit_up(gi):
            tcn, G = groups[gi]
            ws = gi % 2
            for f4 in range(4):
                fch = G * 4 + f4
                w0 = pfl[:, 48 + fch * 3:49 + fch * 3]
                w1 = pfl[:, 49 + fch * 3:50 + fch * 3]
                w2 = pfl[:, 50 + fch * 3:51 + fch * 3]
                bb = pfl[:, 120 + fch:121 + fch]
                for sub in range(3):
                    o0, o1 = SUBO[sub], SUBO[sub + 1]
                    sz = o1 - o0
                    s0 = tcn * 1024 + o0
                    us = ucnt[0] % 2
                    ucnt[0] += 1
                    ua, ul = ps[2 * us][:, 0:sz + 2], ps[2 * us + 1][:, 0:sz]
                    tlo, thi = max((s0 - 1) // 128, 0), min((s0 + sz) // 128, NT - 1)
                    rh = [r_h2T[q] for q in range(tlo, thi + 1)] + [r_h2pad]
                    for k in range(8):
                        mm(ua, wua[ws][:, k, f4 * 128:(f4 + 1) * 128], h2T[:, k, s0:s0 + sz + 2], k == 0, k == 7,
                           rh + [r_wu[ws]], [psr[2 * us]])
                    for k in range(8):
                        mm(ul, wul[ws][:, k, f4 * 128:(f4 + 1) * 128], h2T[:, k, s0 + 1:s0 + 1 + sz], k == 0, k == 7,
                           rh + [r_wl[ws]], [psr[2 * us + 1]])
                    cb, gb = cbuf[us][:, 0:sz], gbuf[us][:, 0:sz]
                    act(AF.Identity, cb, ua[:, 1:sz + 1], [psr[2 * us], r_pf], [r_cbuf[us]], scale=w1, bias=bb)
                    dve("scalar_tensor_tensor", [psr[2 * us], r_pf, r_cbuf[us]], [r_cbuf[us]], out=cb,
                        in0=ua[:, 0:sz], scalar=w0, in1=cb, op0=ALU.mult, op1=ALU.add)
                    dve("scalar_tensor_tensor", [psr[2 * us], r_pf, r_cbuf[us]], [r_cbuf[us]], out=cb,
                        in0=ua[:, 2:sz + 2], scalar=w2, in1=cb, op0=ALU.mult, op1=ALU.add)
                    act(AF.Gelu_apprx_tanh, gb, cb, [r_cbuf[us]], [r_gbuf[us]])
                    dve("tensor_tensor", [r_gbuf[us], psr[2 * us + 1]], [r_actT[ws]],
                        out=actT[ws][:, f4, o0:o1], in0=gb, in1=ul, op=ALU.mult)

        def emit_down(gi):
            tcn, G = groups[gi]
            ws = gi % 2
            for t in range(8):
                for c in range(2):
                    pb = 5 + pycnt[0] % 2
                    pycnt[0] += 1
                    cs = slice(c * 512, (c + 1) * 512)
                    for f4 in range(4):
                        mm(ps[pb], actT[ws][:, f4, t * 128:(t + 1) * 128], wd[ws][:, f4, cs], f4 == 0, f4 == 3,
                           [r_actT[ws], r_wd[ws]], [psr[pb]])
                    dve("tensor_tensor", [psr[pb], r_xacc[t]], [r_xacc[t]], out=xacc[:, t, cs], in0=ps[pb],
                        in1=xacc[:, t, cs], op=ALU.add)

        def emit_xload_tile(tcn, t):
            i = 8 * tcn + t
            dma("sp", "xacc%d" % t, xacc[:, t, :], xs_tiles[i], [r_xs[i]], [r_xacc[t]])

        def emit_xload(tcn):
            for t in range(8):
                emit_xload_tile(tcn, t)

        def emit_xstore(tcn, nxt):
            for t in range(8):
                i = 8 * tcn + t
                if not last:
                    dma("sp", "xst%d" % t, xs_tiles[i], xacc[:, t, :], [r_xacc[t]], [r_xs[i]])
                else:
                    sl = t % 2
                    ssx, tmpx, rsx = (stat[:, 4 * sl:4 * sl + 1], stat[:, 4 * sl + 1:4 * sl + 2],
                                      stat[:, 4 * sl + 2:4 * sl + 3])
                    act(AF.Square, cbuf[sl].bitcast(BF16), xacc[:, t, :], [r_xacc[t]], [r_cbuf[sl], r_st[sl]],
                        accum_out=ssx)
                    rstd_from_ss(ssx, rsx, D, r_st[sl], r_st[sl], tmpx, r_st[sl])
                    dve("scalar_tensor_tensor", [r_xacc[t], r_st[sl], r_gA], [r_xacc[t]], out=xacc[:, t, :],
                        in0=xacc[:, t, :], scalar=rsx, in1=gA, op0=ALU.mult, op1=ALU.mult)
                    out_ops.append(dma("sp", "yst%d" % t, y_tiles[i], xacc[:, t, :], [r_xacc[t]], []))
                if nxt:
                    emit_xload_tile(tcn + 1, t)

        emit_wload(0)
        emit_xload(0)
        emit_up(0)
        for gi in range(len(groups)):
            tcn, G = groups[gi]
            if gi + 1 < len(groups):
                emit_wload(gi + 1)
                emit_up(gi + 1)
            emit_down(gi)
            if G == 5:
                emit_xstore(tcn, tcn < 3 and gi + 1 < len(groups))

    P.emit("sp", lambda e: e.nop(), deps=out_ops)
    P.finalize()
    lanes = list(P.lanes.keys())
    semctx = [nc.semaphore("s_" + ln) for ln in lanes]
    sems = {ln: c.__enter__() for ln, c in zip(lanes, semctx)}
    with nc.Block() as block:
        @block.sync
        def _(e):
            P.replay("sp", e, sems)

        @block.gpsimd
        def _(e):
            P.replay("pool", e, sems)

        @block.scalar
        def _(e):
            P.replay("act", e, sems)

        @block.vector
        def _(e):
            P.replay("dve", e, sems)

        @block.tensor
        def _(e):
            P.replay("pe", e, sems)
    for c in reversed(semctx):
        c.__exit__(None, None, None)
    psctx.__exit__(None, None, None)
    ctx.__exit__(None, None, None)
    nops = {k: len(v) for k, v in P.ops.items()}
    return nc, nops


_CACHE = {}


def kernel(**inputs):
    depth = DEPTH
    shared = _prep_shared(inputs, depth)
    shared["idn"] = np.eye(128, dtype=np.float32)
    x = np.asarray(inputs["x"], np.float32)
    nb = x.shape[0]
    if "nc" not in _CACHE:
        _CACHE["nc"] = build(depth)[0]
    nc = _CACHE["nc"]
    in_maps = []
    for b in range(nb):
        m = dict(shared)
        m["x"] = np.ascontiguousarray(x[b])
        in_maps.append(m)
    res = run_bass_kernel_spmd(nc, in_maps, core_ids=list(range(nb)))
    return np.stack([np.asarray(r["y"], np.float32) for r in res.results], axis=0)
```

```python
import numpy as np
import concourse.bass as bass
import concourse.mybir as mybir
from concourse.bass_utils import run_bass_kernel_spmd

F32 = mybir.dt.float32
BF16 = mybir.dt.bfloat16
AF = mybir.ActivationFunctionType
ALU = mybir.AluOpType

D = 1024
S = 4096
NT = S // 128
DEPTH = 4
D_ATT = 512
D_LRU = 512
D_IN = 2560
D_FF = 3072
NH = 8
EPS = 1e-6
NEG = -30000.0
NPF = 144
NBLK = 9


class Res:
    __slots__ = ("name", "w", "r")

    def __init__(self, name=""):
        self.name = name
        self.w = None
        self.r = {}


class Op:
    __slots__ = ("eng", "lane", "fn", "deps", "sig", "cnt", "inc", "idx")

    def __init__(self, eng, lane, fn, inc):
        self.idx = 0
        self.eng = eng
        self.lane = lane
        self.fn = fn
        self.deps = []
        self.sig = False
        self.cnt = 0
        self.inc = inc


class Prog:
    ENGS = ("pe", "act", "dve", "pool", "sp")

    def __init__(self):
        self.ops = {e: [] for e in self.ENGS}
        self.lanes = {}
        self.all_ops = []

    def emit(self, eng, fn, reads=(), writes=(), lane=None, deps=()):
        is_dma = lane is not None
        lane = lane if is_dma else eng
        op = Op(eng, lane, fn, 16 if is_dma else 1)
        op.sig = is_dma
        ds = set()
        for r in reads:
            if r.w is not None:
                ds.add(r.w)
        for w in writes:
            if w.w is not None:
                ds.add(w.w)
            for o in w.r.values():
                ds.add(o)
        for d in deps:
            if d is not None:
                ds.add(d)
        ds.discard(op)
        for d in ds:
            if d.lane == "pe" and op.lane == "pe":
                continue
            op.deps.append(d)
            d.sig = True
        for r in reads:
            r.r[lane] = op
        for w in writes:
            w.w = op
            w.r = {}
        op.idx = len(self.all_ops)
        self.ops[eng].append(op)
        self.lanes.setdefault(lane, []).append(op)
        self.all_ops.append(op)
        return op

    def finalize(self):
        for lane, ops in self.lanes.items():
            c = 0
            for op in ops:
                if op.sig:
                    c += op.inc
                    op.cnt = c

    def replay(self, eng, e, sems):
        waited = {}
        for op in self.ops[eng]:
            need = {}
            for d in op.deps:
                if d.cnt > need.get(d.lane, 0):
                    need[d.lane] = d.cnt
            for lane, c in need.items():
                if waited.get(lane, 0) < c:
                    e.wait_ge(sems[lane], c)
                    waited[lane] = c
            ins = op.fn(e)
            if op.sig:
                ins.then_inc(sems[op.lane], op.inc)


def _mb_index():
    GW, WH, WW = 64, 8, 16
    p = np.arange(128)
    ka, kc = p // 64, p % 64
    qb, qc = p // 64, p % 64
    cs = np.clip(qc - WW // 2, 0, GW - WW)
    colok = (kc[:, None] >= cs[None, :]) & (kc[:, None] < cs[None, :] + WW)
    dc = kc[:, None] - qc[None, :] + (WW - 1)
    blocks = []
    specs = [(-2, True), (-1, True), (0, True), (1, True), (2, True),
             (2, False), (3, False), (-2, False), (-3, False)]
    for dt, banded in specs:
        dr_rel = 2 * dt + ka[:, None] - qb[None, :]
        if banded:
            rowok = (dr_rel >= -4) & (dr_rel <= 3)
        else:
            rowok = np.ones_like(dr_rel, bool)
        ok = rowok & colok
        dr = dr_rel + (WH - 1)
        ok &= (dr >= 0) & (dr <= 2 * WH - 2) & (dc >= 0) & (dc <= 2 * WW - 2)
        blocks.append((np.clip(dr, 0, 14), np.clip(dc, 0, 30), ok))
    return blocks


def _prep_shared(inp, depth):
    f32 = np.float32
    blocks = _mb_index()
    rb = np.asarray(inp["rel_bias"], f32)
    mb = np.empty((depth, 128, NH, NBLK, 128), f32)
    for b, (dr, dc, ok) in enumerate(blocks):
        g = rb[:depth][:, :, dr, dc]
        g = np.where(ok[None, None], g, f32(NEG))
        mb[:, :, :, b, :] = g.transpose(0, 2, 1, 3)
    lw = np.asarray(inp["lru_w"], f32)[:depth]
    bd = np.zeros((depth, 128, 2, 2, 4, 128), f32)
    for cc in range(4):
        for hb in range(2):
            bd[:, hb * 64:(hb + 1) * 64, :, :, cc, hb * 64:(hb + 1) * 64] = \
                lw[:, :, :, 2 * cc + hb].transpose(0, 3, 1, 2, 4)
    bd = bd.reshape(depth, 128, 16 * 128)
    pf = np.zeros((depth, 128, NPF), f32)

    def fm(a, nch):
        return a.reshape(a.shape[:-1] + (nch, 128))

    for l in range(depth):
        cw = fm(np.asarray(inp["conv_lru_w"], f32)[l], 4)
        pf[l, :, 0:16] = cw.transpose(2, 1, 0).reshape(128, 16)
        pf[l, :, 16:20] = fm(np.asarray(inp["conv_lru_b"], f32)[l], 4).T
        lb = fm(np.asarray(inp["lru_b"], f32)[l], 4)
        pf[l, :, 20:36] = lb.transpose(3, 0, 1, 2).reshape(128, 16)
        ll = fm(np.asarray(inp["lru_lam"], f32)[l], 4)
        pf[l, :, 36:44] = ll.transpose(2, 0, 1).reshape(128, 8)
        pf[l, :, 44:48] = fm(np.asarray(inp["g_rec"], f32)[l], 4).T
        fw = fm(np.asarray(inp["conv_ffn_w"], f32)[l], 24)
        pf[l, :, 48:120] = fw.transpose(2, 1, 0).reshape(128, 72)
        pf[l, :, 120:144] = fm(np.asarray(inp["conv_ffn_b"], f32)[l], 24).T
    shared = {
        "w_in": np.ascontiguousarray(np.asarray(inp["w_in"], f32)[:depth]),
        "w_out": np.ascontiguousarray(np.asarray(inp["w_out"], f32)[:depth]),
        "w_up": np.ascontiguousarray(np.asarray(inp["w_up"], f32)[:depth]),
        "w_down": np.ascontiguousarray(np.asarray(inp["w_down"], f32)[:depth]),
        "g_mix": np.ascontiguousarray(np.asarray(inp["g_mix"], f32)[:depth]),
        "g_ffn": np.ascontiguousarray(np.asarray(inp["g_ffn"], f32)[:depth]),
        "g_att": np.ascontiguousarray(np.asarray(inp["g_att"], f32)[:depth]),
        "g_final": np.ascontiguousarray(np.asarray(inp["g_final"], f32)).reshape(1, D),
        "mb": mb.reshape(depth, 128, NH * NBLK * 128),
        "bd": bd,
        "pf": pf,
    }
    return shared


KW = 256
ARENA_KIB = 204


def _mk(fn, *a, **k):
    return lambda e: fn(e, *a, **k)


def build(depth=DEPTH, dbg=False):
    nc = bass.Bass("TRN2", target_bir_lowering=False)
    dt = nc.dram_tensor
    x_d = dt("x", [S, D], F32, kind="ExternalInput").ap()
    w_in_d = dt("w_in", [depth, D, D_IN], F32, kind="ExternalInput").ap()
    w_out_d = dt("w_out", [depth, D, D], F32, kind="ExternalInput").ap()
    w_up_d = dt("w_up", [depth, D, 2 * D_FF], F32, kind="ExternalInput").ap()
    w_down_d = dt("w_down", [depth, D_FF, D], F32, kind="ExternalInput").ap()
    g_mix_d = dt("g_mix", [depth, D], F32, kind="ExternalInput").ap()
    g_ffn_d = dt("g_ffn", [depth, D], F32, kind="ExternalInput").ap()
    g_att_d = dt("g_att", [depth, D_ATT], F32, kind="ExternalInput").ap()
    g_fin_d = dt("g_final", [1, D], F32, kind="ExternalInput").ap()
    mb_d = dt("mb", [depth, 128, NH * NBLK * 128], F32, kind="ExternalInput").ap()
    bd_d = dt("bd", [depth, 128, 16 * 128], F32, kind="ExternalInput").ap()
    pf_d = dt("pf", [depth, 128, NPF], F32, kind="ExternalInput").ap()
    idn_d = dt("idn", [128, 128], F32, kind="ExternalInput").ap()
    y_d = dt("y", [S, D], F32, kind="ExternalOutput").ap()
    xs_d = dt("xs", [S, D], F32, kind="Internal").ap()
    zr_d = dt("zr", [1024, S], F32, kind="Internal").ap()
    dbg_t = {}
    if dbg:
        dbg_t["hT"] = dt("d_hT", [128, 8 * S], BF16, kind="ExternalOutput").ap()
        dbg_t["qT"] = dt("d_qT", [128, 4 * S], BF16, kind="ExternalOutput").ap()
        dbg_t["kT"] = dt("d_kT", [128, 4 * S], BF16, kind="ExternalOutput").ap()
        dbg_t["V"] = dt("d_V", [128, NT * NH * 65], BF16, kind="ExternalOutput").ap()
        dbg_t["attT"] = dt("d_attT", [128, 4 * S], BF16, kind="ExternalOutput").ap()
        dbg_t["recT"] = dt("d_recT", [128, 4 * S], BF16, kind="ExternalOutput").ap()
        dbg_t["ssa"] = dt("d_ssa", [128, 64], F32, kind="ExternalOutput").ap()
        dbg_t["zr"] = dt("d_zr", [1024, S], F32, kind="ExternalOutput").ap()
        dbg_t["xD"] = dt("d_xD", [S, D], F32, kind="ExternalOutput").ap()

    P = Prog()
    ctx = nc.sbuf_tensor("arena", [128, ARENA_KIB * KW], F32)
    arena = ctx.__enter__()
    psctx = nc.psum_tensor("psall", [128, 4096], F32)
    ps_all = psctx.__enter__()
    ps = [ps_all[:, b * 512:(b + 1) * 512] for b in range(8)]
    psr = [Res("ps%d" % b) for b in range(8)]

    def reg(off_w, n_w, dtype=F32):
        a = arena[:, off_w:off_w + n_w]
        return a.bitcast(BF16) if dtype == BF16 else a

    def kib(k):
        return int(round(k * KW))

    ident = reg(0, 64, BF16)
    ones = reg(64, 1)
    pf_all = reg(72, depth * NPF).rearrange("p (l f) -> p l f", l=depth)
    drv = reg(648, 64)
    nch = drv[:, 0:8]
    bh = drv[:, 8:24]
    cwh = drv[:, 24:44]
    nch2 = drv[:, 44:52]
    ser = reg(712, 64)
    stat = reg(776, 64)
    ss_att = reg(840, 32)
    rstd_att = reg(872, 32)
    rstd_rec = reg(904, 32)
    gA = reg(1024, 1024)
    gB = reg(2048, 512)
    r_ident, r_ones, r_pf, r_drv, r_gA, r_gB = (Res(n) for n in ("ident", "ones", "pf", "drv", "gA", "gB"))
    r_ssatt, r_rstda, r_rstdr = Res("ssatt"), Res("rstda"), Res("rstdr")

    O_Q, O_K, O_V, O_X1, O_AT = kib(10), kib(42), kib(74), kib(106.5), kib(170.5)
    qT = reg(O_Q, kib(32), BF16).rearrange("p (c t) -> p c t", c=4)
    kT = reg(O_K, kib(32), BF16).rearrange("p (c t) -> p c t", c=4)
    Vt = reg(O_V, NT * NH * 65 // 2, BF16).rearrange("p (t h d) -> p t h d", t=NT, h=NH)
    hT = reg(O_X1, kib(64), BF16).rearrange("p (k t) -> p k t", k=8)
    attT = reg(O_AT, kib(32), BF16).rearrange("p (c t) -> p c t", c=4)
    recT = reg(O_Q, kib(32), BF16).rearrange("p (c t) -> p c t", c=4)

    def fence(res_list):
        f = {}
        for r in res_list:
            for o in ([r.w] if r.w is not None else []) + list(r.r.values()):
                if o.lane not in f or o.idx > f[o.lane].idx:
                    f[o.lane] = o
        return f

    def fenced(name, f):
        r = Res(name)
        r.r = dict(f)
        return r

    users = {rg: [] for rg in ("Q", "K", "V", "X1", "AT")}

    def phase_begin(regs):
        f = {}
        for rg in regs:
            for ln_, o in fence(users[rg]).items():
                if ln_ not in f or o.idx > f[ln_].idx:
                    f[ln_] = o
            users[rg] = []
        return f

    def newres(name, f, regs):
        r = fenced(name, f)
        for rg in regs:
            users[rg].append(r)
        return r

    lane_names = []

    def lane(name):
        if name not in lane_names:
            lane_names.append(name)
        return name

    def act(fn_, out, in_, reads, writes, **kw):
        return P.emit("act", lambda e: e.activation(out=out, in_=in_, func=fn_, **kw), reads, writes)

    def dve(method, reads, writes, **kw):
        return P.emit("dve", lambda e: getattr(e, method)(**kw), reads, writes)

    def pool(method, reads, writes, **kw):
        return P.emit("pool", lambda e: getattr(e, method)(**kw), reads, writes)

    def mm(out, lhsT, rhs, start, stop, reads, writes):
        return P.emit("pe", lambda e: e.matmul(out, lhsT, rhs, start=start, stop=stop), reads, writes)

    def tr(out, in_, reads, writes):
        return P.emit("pe", lambda e: e.transpose(out=out, in_=in_, identity=ident), reads + [r_ident], writes)

    last_pool_dma = [None]

    def dma(eng, ln, out, in_, reads, writes):
        if eng == "pool":
            op = P.emit(eng, lambda e: e.dma_start(out=out, in_=in_), reads, writes, lane=lane(ln),
                        deps=[last_pool_dma[0]])
            last_pool_dma[0] = op
            return op
        return P.emit(eng, lambda e: e.dma_start(out=out, in_=in_), reads, writes, lane=lane(ln))

    def rstd_from_ss(ss_ap, out_ap, n, r_in, r_out, tmp_ap, r_tmp):
        act(AF.Ln, tmp_ap, ss_ap, [r_in], [r_tmp], scale=1.0 / n, bias=EPS)
        act(AF.Exp, out_ap, tmp_ap, [r_tmp], [r_out], scale=-0.5)

    dma("pool", "setup_p", ident, idn_d, [], [r_ident])
    dve("memset", [], [r_ones], ap=ones, constant=1.0)
    dma("sp", "setup", pf_all, pf_d.rearrange("l p f -> p l f"), [], [r_pf])

    r_xs = [Res("xs%d" % i) for i in range(NT)]
    r_zr = [Res("zr%d" % i) for i in range(8)]
    x_tiles = x_d.rearrange("(t p) d -> t p d", p=128)
    xs_tiles = xs_d.rearrange("(t p) d -> t p d", p=128)
    y_tiles = y_d.rearrange("(t p) d -> t p d", p=128)
    out_ops = []

    for l in range(depth):
        src_tiles = x_tiles if l == 0 else xs_tiles
        last = (l == depth - 1)
        dma("sp", "gA", gA, g_mix_d[l:l + 1, :].partition_broadcast(128), [], [r_gA])
        dma("sp", "gB", gB, g_att_d[l:l + 1, :].partition_broadcast(128), [], [r_gB])
        lam = pf_all[:, l, 36:44]
        s_ax, s_y, s_t, s_z, s_z2, s_p, s_m = (ser[:, 8 * i:8 * i + 8] for i in range(7))
        r_s = [Res("ser%d" % i) for i in range(7)]
        act(AF.Abs, s_ax, lam, [r_pf], [r_s[0]])
        act(AF.Exp, s_y, s_ax, [r_s[0]], [r_s[1]], scale=-1.0)
        dve("tensor_scalar", [r_s[1]], [r_s[2]], out=s_t, in0=s_y, scalar1=2.0, scalar2=None, op0=ALU.add)
        dve("reciprocal", [r_s[2]], [r_s[2]], out=s_t, in_=s_t)
        dve("tensor_tensor", [r_s[1], r_s[2]], [r_s[3]], out=s_z, in0=s_y, in1=s_t, op=ALU.mult)
        dve("tensor_tensor", [r_s[3]], [r_s[4]], out=s_z2, in0=s_z, in1=s_z, op=ALU.mult)
        dve("tensor_scalar", [r_s[4]], [r_s[5]], out=s_p, in0=s_z2, scalar1=1.0 / 9, scalar2=1.0 / 7,
            op0=ALU.mult, op1=ALU.add)
        for cst in (1.0 / 5, 1.0 / 3, 1.0):
            dve("tensor_tensor", [r_s[5], r_s[4]], [r_s[5]], out=s_p, in0=s_p, in1=s_z2, op=ALU.mult)
            dve("tensor_scalar", [r_s[5]], [r_s[5]], out=s_p, in0=s_p, scalar1=cst, scalar2=None, op0=ALU.add)
        dve("scalar_tensor_tensor", [r_s[3], r_s[5]], [r_s[5]], out=s_p, in0=s_z, scalar=2.0, in1=s_p,
            op0=ALU.mult, op1=ALU.mult)
        act(AF.Relu, s_m, lam, [r_pf], [r_s[6]], scale=-1.0)
        dve("tensor_tensor", [r_s[5], r_s[6]], [r_s[6]], out=s_m, in0=s_m, in1=s_p, op=ALU.add)
        dve("tensor_scalar", [r_s[6]], [r_drv], out=nch, in0=s_m, scalar1=-4.0, scalar2=None, op0=ALU.mult)
        dve("tensor_scalar", [r_s[6]], [r_drv], out=nch2, in0=s_m, scalar1=-8.0, scalar2=None, op0=ALU.mult)
        dve("tensor_scalar", [r_pf], [r_drv], out=bh, in0=pf_all[:, l, 20:36], scalar1=0.5, scalar2=None,
            op0=ALU.mult)
        dve("tensor_scalar", [r_pf], [r_drv], out=cwh, in0=pf_all[:, l, 0:20], scalar1=0.5, scalar2=None,
            op0=ALU.mult)

        O_W = O_AT
        xt = [reg(O_W + kib(4) * i, kib(4)) for i in range(4)]
        ht = [reg(O_W + kib(16) + kib(2) * i, kib(2), BF16) for i in range(3)]
        junk = reg(O_W + kib(22), kib(2), BF16)
        tmpP = [reg(O_W + kib(24) + kib(4) * i, kib(4)) for i in range(2)]
        fA = phase_begin(["Q", "K", "V", "X1", "AT"])
        r_xt = [newres("xt%d" % i, fA, ["AT"]) for i in range(4)]
        r_ht = [newres("ht%d" % i, fA, ["AT"]) for i in range(3)]
        r_junk = newres("junk", fA, ["AT"])
        r_tmpP = [newres("tmpP%d" % i, fA, ["AT"]) for i in range(2)]
        r_st = [Res("st%d" % i) for i in range(4)]
        r_hT = [newres("hT%d" % i, fA, ["X1"]) for i in range(NT)]
        def A0_S0(i):
            sl = i % 4
            dma("sp", "xt%d" % sl, xt[sl], src_tiles[i], [r_xs[i]], [r_xt[sl]])

        def A0_S1(i):
            sl, s2 = i % 4, i % 2
            ssx, tmpx, rsx = stat[:, 4 * s2:4 * s2 + 1], stat[:, 4 * s2 + 1:4 * s2 + 2], stat[:, 4 * s2 + 2:4 * s2 + 3]
            act(AF.Square, junk, xt[sl], [r_xt[sl]], [r_junk, r_st[s2]], accum_out=ssx)
            rstd_from_ss(ssx, rsx, D, r_st[s2], r_st[s2], tmpx, r_st[s2])

        def A0_S2(i):
            sl, hl, s2 = i % 4, i % 3, i % 2
            rsx = stat[:, 4 * s2 + 2:4 * s2 + 3]
            dve("scalar_tensor_tensor", [r_xt[sl], r_st[s2], r_gA], [r_ht[hl]], out=ht[hl], in0=xt[sl],
                scalar=rsx, in1=gA, op0=ALU.mult, op1=ALU.mult)

        def A0_S3(i):
            hl, pb = i % 3, i % 2
            pst = ps[pb].bitcast(BF16)
            for k in range(8):
                tr(pst[:, k * 128:(k + 1) * 128], ht[hl][:, k * 128:(k + 1) * 128], [r_ht[hl]], [psr[pb]])

        def A0_S4(i):
            pb = i % 2
            pst = ps[pb].bitcast(BF16)
            o = hT[:, :, i * 128:(i + 1) * 128]
            s_ = pst.rearrange("p (k t) -> p k t", k=8)
            if i % 2 == 0:
                act(AF.Copy, o, s_, [psr[pb]], [r_hT[i]])
            else:
                dve("tensor_copy", [psr[pb]], [r_hT[i]], out=o, in_=s_)

        for i_ in range(3):
            A0_S0(i_)
        for t_ in range(NT + 3):
            if t_ < NT:
                A0_S1(t_)
            if 0 <= t_ - 1 < NT:
                A0_S2(t_ - 1)
            if t_ + 3 < NT:
                A0_S0(t_ + 3)
            if 0 <= t_ - 2 < NT:
                A0_S3(t_ - 2)
            if 0 <= t_ - 3 < NT:
                A0_S4(t_ - 3)
        if dbg and l == 0:
            out_ops.append(dma("sp", "dbgx1", dbg_t["hT"], hT.rearrange("p k t -> p (k t)"), r_hT, []))

        f_w = fence(r_xt + r_ht + [r_junk] + r_tmpP)
        wb = [reg(O_W + kib(8) * i, kib(8), BF16).rearrange("p (k c) -> p k c", k=8) for i in range(2)]
        r_wb = [newres("wb%d" % i, f_w, ["AT"]) for i in range(2)]
        stg = [reg(O_W + kib(16) + kib(2) * i, kib(2)) for i in range(4)]
        r_stg = [newres("stg%d" % i, f_w, ["AT"]) for i in range(4)]
        r_qT = [newres("qT%d" % i, fA, ["Q"]) for i in range(8)]
        r_kT = [newres("kT%d" % i, fA, ["K"]) for i in range(8)]
        r_V = [newres("V%d" % i, fA, ["V"]) for i in range(NT)]
        r_Vones = newres("Vones", fA, ["V"])
        pool("memset", [], [r_Vones], ap=Vt[:, :, :, 64:65], constant=1.0)
        w_l = w_in_d[l].rearrange("(k p) c -> p k c", p=128)
        nev = 0
        bank = 0
        stg_i = 0
        for g in range(5):
            sl = g % 2
            dma("pool", "wb%d" % sl, wb[sl], w_l[:, :, g * 512:(g + 1) * 512], [], [r_wb[sl]])
            if g == 2:
                for i in range(NT):
                    b = bank % 8
                    bank += 1
                    for k in range(8):
                        mm(ps[b], hT[:, k, i * 128:(i + 1) * 128], wb[sl][:, k, :], k == 0, k == 7,
                           [r_hT[i], r_wb[sl]], [psr[b]])
                    o = Vt[:, i, :, 0:64]
                    s_ = ps[b].rearrange("p (h d) -> p h d", h=NH)
                    if nev % 2 == 0:
                        act(AF.Copy, o, s_, [psr[b]], [r_V[i]])
                    else:
                        dve("tensor_copy", [psr[b]], [r_V[i]], out=o, in_=s_)
                    nev += 1
                continue
            for c in range(4):
                for tc in range(8):
                    b = bank % 8
                    bank += 1
                    rd = [r_hT[4 * tc + j] for j in range(4)] + [r_wb[sl]]
                    for k in range(8):
                        mm(ps[b], wb[sl][:, k, c * 128:(c + 1) * 128], hT[:, k, tc * 512:(tc + 1) * 512],
                           k == 0, k == 7, rd, [psr[b]])
                    if g < 2:
                        dstT, rr = (qT, r_qT) if g == 0 else (kT, r_kT)
                        o = dstT[:, c, tc * 512:(tc + 1) * 512]
                        sc = 0.125 if g == 0 else 1.0
                        if nev % 2 == 0:
                            act(AF.Copy, o, ps[b], [psr[b]], [rr[tc]], scale=sc)
                        else:
                            dve("tensor_scalar", [psr[b]], [rr[tc]], out=o, in0=ps[b], scalar1=sc, scalar2=None,
                                op0=ALU.mult)
                    else:
                        ss_ = stg_i % 4
                        stg_i += 1
                        if nev % 2 == 0:
                            act(AF.Copy, stg[ss_], ps[b], [psr[b]], [r_stg[ss_]])
                        else:
                            dve("tensor_copy", [psr[b]], [r_stg[ss_]], out=stg[ss_], in_=ps[b])
                        row = (g - 3) * 4 + c
                        dma("sp", "stg%d" % ss_, zr_d[row * 128:(row + 1) * 128, tc * 512:(tc + 1) * 512], stg[ss_],
                            [r_stg[ss_]], [r_zr[row]])
                    nev += 1
        if dbg and l == 0:
            out_ops.append(dma("sp", "dbgx2", dbg_t["qT"], qT.rearrange("p c t -> p (c t)"), r_qT, []))
            out_ops.append(dma("sp", "dbgx3", dbg_t["kT"], kT.rearrange("p c t -> p (c t)"), r_kT, []))
            out_ops.append(dma("sp", "dbgx4", dbg_t["V"], Vt.rearrange("p t h d -> p (t h d)"), r_V + [r_Vones], []))
            out_ops.append(dma("sp", "dbgx5", dbg_t["zr"], zr_d, r_zr, []))
        if dbg == "A":
            break

        fB = phase_begin(["X1", "AT"])
        mbt = reg(O_X1, kib(18), BF16).rearrange("p (h b q) -> p h b q", h=NH, b=NBLK)
        r_mb = newres("mb", fB, ["X1"])
        dma("pool", "mb", reg(O_X1, kib(18), BF16), mb_d[l], [], [r_mb])
        ob = O_X1 + kib(18)
        qz = [reg(ob + 512 * i, 512, BF16).rearrange("p (h q) -> p h q", h=NH) for i in range(2)]
        pTb = [reg(ob + 1024 + 320 * i, 320, BF16) for i in range(2)]
        attb = [reg(ob + 1920 + 512 * i, 512) for i in range(2)]
        attn = [reg(ob + 2944 + 256 * i, 256, BF16) for i in range(2)]
        junkb = reg(ob + 3456, 256, BF16)
        rsb = reg(ob + 3712, 16)
        r_qz = [newres("qz%d" % i, fB, ["X1"]) for i in range(2)]
        r_pTb = [newres("pTb%d" % i, fB, ["X1"]) for i in range(2)]
        r_attb = [newres("attb%d" % i, fB, ["X1"]) for i in range(2)]
        r_attn = [newres("attn%d" % i, fB, ["X1"]) for i in range(2)]
        r_junkb = newres("junkb", fB, ["X1"])
        r_rsb = [newres("rsb%d" % i, fB, ["X1"]) for i in range(2)]
        r_attT = [newres("attT%d" % i, fB, ["AT"]) for i in range(NT)]
        for i in range(2):
            pool("memset", [], [r_qz[i]], ap=qz[i], constant=0.0)

        def tile_plan(j):
            if j == 0:
                return [0, 1, 2, 3], [2, 3, 5, 6]
            if j == 1:
                return [0, 1, 2, 3], [1, 2, 3, 5]
            if j == NT - 2:
                return [28, 29, 30, 31], [7, 1, 2, 3]
            if j == NT - 1:
                return [28, 29, 30, 31], [8, 7, 1, 2]
            return [j - 2, j - 1, j, j + 1, j + 2], [0, 1, 2, 3, 4]

        def runs(bl):
            out, st = [], 0
            for i in range(1, len(bl) + 1):
                if i == len(bl) or bl[i] != bl[i - 1] + 1:
                    out.append((st, i))
                    st = i
            return out

        units = [(j, h) for j in range(NT) for h in range(NH)]

        def spv(s):
            return ps_all[:, s * 1024:s * 1024 + 1024].rearrange("p (n q) -> p n q", n=8)

        def r_sp(s):
            return [psr[2 * s], psr[2 * s + 1]]

        def emit_qz(j):
            qs = j % 2
            ts_ = slice(j * 128, (j + 1) * 128)
            pool("tensor_copy", [r_qT[j // 4]], [r_qz[qs]], out=qz[qs][0:64, 0::2, :], in_=qT[0:64, :, ts_])
            pool("tensor_copy", [r_qT[j // 4]], [r_qz[qs]], out=qz[qs][64:128, 1::2, :], in_=qT[64:128, :, ts_])

        def emit_qk(u):
            j, h = units[u]
            tiles, blks = tile_plan(j)
            s = u % 2
            c = h // 2
            if u == 0:
                emit_qz(0)
            if h == 3 and j + 1 < NT:
                emit_qz(j + 1)
            for n, (t, b_) in enumerate(zip(tiles, blks)):
                pr_ = [psr[2 * s + n // 4]]
                mm(spv(s)[:, n, :], kT[:, c, t * 128:(t + 1) * 128], qz[j % 2][:, h, :], True, False,
                   [r_kT[t // 4], r_qz[j % 2]], pr_)
                mm(spv(s)[:, n, :], ident, mbt[:, h, b_, :], False, True, [r_ident, r_mb], pr_)

        def emit_exp(u):
            j, h = units[u]
            tiles, blks = tile_plan(j)
            nb = len(tiles)
            s = u % 2
            act(AF.Exp, pTb[s][:, 0:nb * 128], ps_all[:, s * 1024:s * 1024 + nb * 128],
                [psr[2 * s], psr[2 * s + 1]], [r_pTb[s]])

        def emit_pv(u):
            j, h = units[u]
            tiles, blks = tile_plan(j)
            nb = len(tiles)
            s = u % 2
            ob_ = 4 + h // 4
            for n, t in enumerate(tiles):
                mm(ps[ob_][:, (h % 4) * 128:(h % 4) * 128 + 65], pTb[s][:, n * 128:(n + 1) * 128], Vt[:, t, h, :],
                   n == 0, n == nb - 1, [r_pTb[s], r_V[t], r_Vones], [psr[ob_]])
            if h % 4 == 3:
                hf = h // 4
                sl = j % 2
                opv = ps[ob_].rearrange("p (g d) -> p g d", g=4)
                dve("reciprocal", [psr[ob_]], [r_rsb[hf]], out=rsb[:, 4 * hf:4 * hf + 4], in_=opv[:, :, 64])
                dve("tensor_tensor", [psr[ob_], r_rsb[hf]], [r_attb[sl]],
                    out=attb[sl][:, hf * 256:(hf + 1) * 256].rearrange("p (g d) -> p g d", g=4),
                    in0=opv[:, :, 0:64], in1=rsb[:, 4 * hf:4 * hf + 4].unsqueeze(2).to_broadcast([128, 4, 64]),
                    op=ALU.mult)

        def emit_post(j):
            sl = j % 2
            act(AF.Square, junkb, attb[sl], [r_attb[sl]], [r_junkb, r_ssatt], accum_out=ss_att[:, j:j + 1])
            dve("tensor_tensor", [r_attb[sl], r_gB], [r_attn[sl]], out=attn[sl], in0=attb[sl], in1=gB, op=ALU.mult)

        def emit_tr(j):
            sl = j % 2
            pb = 6 + j % 2
            pst = ps[pb].bitcast(BF16)
            for k in range(4):
                tr(pst[:, k * 128:(k + 1) * 128], attn[sl][:, k * 128:(k + 1) * 128], [r_attn[sl]], [psr[pb]])
            act(AF.Copy, attT[:, :, j * 128:(j + 1) * 128], pst[:, 0:512].rearrange("p (k t) -> p k t", k=4),
                [psr[pb]], [r_attT[j]])

        def after_pv(u):
            j, h = units[u]
            if h == NH - 1:
                emit_post(j)
            if h == 1 and j > 0:
                emit_tr(j - 1)

        emit_qk(0)
        for u in range(len(units)):
            if u + 1 < len(units):
                emit_qk(u + 1)
            emit_exp(u)
            if u >= 1:
                emit_pv(u - 1)
                after_pv(u - 1)
        emit_pv(len(units) - 1)
        after_pv(len(units) - 1)
        emit_tr(NT - 1)
        if dbg and l == 0:
            out_ops.append(dma("sp", "dbgy1", dbg_t["attT"], attT.rearrange("p c t -> p (c t)"), r_attT, []))
            out_ops.append(dma("sp", "dbgy2", dbg_t["ssa"][:, 0:32], ss_att, [r_ssatt], []))
        if dbg == "B":
            break

        fC = phase_begin(["Q", "K", "V", "X1"])
        o = O_K
        xr = reg(o, 4104); o += 4104
        xch = reg(o, 4096); o += 4096
        xcb = reg(o, 2048, BF16); o += 2048
        Hf = reg(o, 4096); o += 4096
        Ab = reg(o, 4096); o += 4096
        Ib = reg(o, 4096); o += 4096
        Tb = reg(o, 4096); o += 4096
        hbk = [reg(o + 512 * i, 512) for i in range(2)]; o += 1024
        sqb = [reg(o + 512 * i, 512) for i in range(2)]; o += 1024
        bdt = reg(o, 1024, BF16).rearrange("p (m c) -> p m c", m=16); o += 1024
        assert o <= O_AT
        gate = xr[:, 2:4098]
        LR = ["K", "V", "X1"]
        r_xr = newres("xr", fC, LR)
        r_xrpad = newres("xrpad", fC, LR)
        r_xch = [newres("xch%d" % i, fC, LR) for i in range(2)]
        r_xcb = [newres("xcb%d" % i, fC, LR) for i in range(2)]
        r_Hf = [newres("Hf%d" % i, fC, LR) for i in range(8)]
        r_A = [newres("A%d" % i, fC, LR) for i in range(8)]
        r_I = [newres("I%d" % i, fC, LR) for i in range(8)]
        r_T = [newres("T%d" % i, fC, LR) for i in range(2)]
        r_hbk = [newres("hbk%d" % i, fC, LR) for i in range(2)]
        r_sqb = [newres("sqb%d" % i, fC, LR) for i in range(2)]
        r_bd = newres("bd", fC, LR)
        r_recT = [newres("recT%d" % i, fC, ["Q"]) for i in range(8)]
        dma("pool", "bd", bdt.rearrange("p m c -> p (m c)"), bd_d[l], [], [r_bd])
        pool("memset", [], [r_xrpad], ap=xr[:, 0:2], constant=0.0)
        pool("memset", [], [r_xrpad], ap=xr[:, 4098:4104], constant=0.0)
        pfl = pf_all[:, l, :]
        gcnt = 0
        for cc in range(4):
            dma("sp", "xr", xr[:, 2:4098], zr_d[cc * 128:(cc + 1) * 128, :], [r_zr[cc]], [r_xr])
            for hf in range(2):
                c0 = hf * 2048
                act(AF.Identity, xch[:, c0:c0 + 2048], xr[:, 2 + c0:2 + c0 + 2048], [r_xr, r_xrpad, r_drv], [r_xch[hf]],
                    scale=cwh[:, cc * 4 + 2:cc * 4 + 3], bias=cwh[:, 16 + cc:17 + cc])
            for hf in range(2):
                c0 = hf * 2048
                dst = xch[:, c0:c0 + 2048]
                for tap in (0, 1, 3):
                    dve("scalar_tensor_tensor", [r_xr, r_xrpad, r_drv, r_xch[hf]], [r_xch[hf]], out=dst,
                        in0=xr[:, tap + c0:tap + c0 + 2048], scalar=cwh[:, cc * 4 + tap:cc * 4 + tap + 1], in1=dst,
                        op0=ALU.mult, op1=ALU.add)
            for hf in range(2):
                c0 = hf * 2048
                act(AF.Copy, xcb[:, c0:c0 + 2048], xch[:, c0:c0 + 2048], [r_xch[hf]], [r_xcb[hf]], scale=2.0)
            dma("sp", "xr", gate, zr_d[512 + cc * 128:512 + (cc + 1) * 128, :], [r_zr[4 + cc]], [r_xr])
            for d_ in range(2):
                order = list(range(8))
                for tc in order:
                    s = gcnt % 2
                    gcnt += 1
                    cs = slice(tc * 512, (tc + 1) * 512)
                    mm(ps[2 * s], bdt[:, d_ * 8 + cc, :], xcb[:, cs], True, True, [r_bd, r_xcb[tc // 4]], [psr[2 * s]])
                    mm(ps[2 * s + 1], bdt[:, d_ * 8 + 4 + cc, :], xcb[:, cs], True, True, [r_bd, r_xcb[tc // 4]],
                       [psr[2 * s + 1]])
                    act(AF.Tanh, Ab[:, cs], ps[2 * s], [psr[2 * s], r_drv], [r_A[tc]], scale=0.5,
                        bias=bh[:, d_ * 8 + cc:d_ * 8 + cc + 1])
                    act(AF.Tanh, Ib[:, cs], ps[2 * s + 1], [psr[2 * s + 1], r_drv], [r_I[tc]], scale=0.5,
                        bias=bh[:, d_ * 8 + 4 + cc:d_ * 8 + 5 + cc])
                ncol = nch[:, d_ * 4 + cc:d_ * 4 + cc + 1]
                ncol2 = nch2[:, d_ * 4 + cc:d_ * 4 + cc + 1]
                hord = [0, 1]
                for hf in hord:
                    hs = slice(hf * 2048, hf * 2048 + 2048)
                    rA = r_A[4 * hf:4 * hf + 4]
                    act(AF.Exp, Tb[:, hs], Ab[:, hs], rA + [r_drv], [r_T[hf]], scale=ncol2, bias=ncol2)
                    act(AF.Exp, Ab[:, hs], Ab[:, hs], rA + [r_drv], rA, scale=ncol, bias=ncol)
                    rI = r_I[4 * hf:4 * hf + 4]
                    dve("scalar_tensor_tensor", rI + [r_xch[hf]], rI, out=Ib[:, hs], in0=Ib[:, hs], scalar=1.0,
                        in1=xch[:, hs], op0=ALU.add, op1=ALU.mult)
                for hf in hord:
                    hs = slice(hf * 2048, hf * 2048 + 2048)
                    rA, rI = r_A[4 * hf:4 * hf + 4], r_I[4 * hf:4 * hf + 4]
                    act(AF.Sqrt, Tb[:, hs], Tb[:, hs], [r_T[hf]], [r_T[hf]], scale=-1.0, bias=1.0)
                    dve("tensor_tensor", rI + [r_T[hf]], rI, out=Ib[:, hs], in0=Ib[:, hs], in1=Tb[:, hs], op=ALU.mult)
                    if d_ == 0:
                        init = 0.0 if hf == 0 else Hf[:, 2047:2048]
                        dve("tensor_tensor_scan", rA + rI + (r_Hf[0:4] if hf == 1 else []), r_Hf[4 * hf:4 * hf + 4],
                            out=Hf[:, hs], data0=Ab[:, hs], data1=Ib[:, hs], initial=init, op0=ALU.mult, op1=ALU.add)
                if d_ == 0:
                    act(AF.Gelu_apprx_tanh, gate, gate, [r_xr], [r_xr])
                else:
                    for tc in range(7, -1, -1):
                        s = tc % 2
                        cs = slice(tc * 512, (tc + 1) * 512)
                        init = 0.0 if tc == 7 else hbk[1 - s][:, 0:1]
                        dve("tensor_tensor_scan", [r_A[tc], r_I[tc], r_hbk[1 - s]], [r_hbk[s]],
                            out=hbk[s][:, ::-1], data0=Ab[:, cs][:, ::-1], data1=Ib[:, cs][:, ::-1], initial=init,
                            op0=ALU.mult, op1=ALU.add)
                        dve("tensor_tensor", [r_Hf[tc], r_hbk[s]], [r_Hf[tc]], out=Hf[:, cs], in0=Hf[:, cs], in1=hbk[s],
                            op=ALU.add)
                        dve("tensor_tensor", [r_Hf[tc], r_xr], [r_Hf[tc]], out=Hf[:, cs], in0=Hf[:, cs], in1=gate[:, cs],
                            op=ALU.mult)
                        act(AF.Identity, recT[:, cc, cs], Hf[:, cs], [r_Hf[tc], r_pf], [r_recT[tc]],
                            scale=pfl[:, 44 + cc:45 + cc])
                        act(AF.Square, sqb[s], Hf[:, cs], [r_Hf[tc]], [r_sqb[s]])
                        for ti in range(4):
                            tok = tc * 4 + ti
                            col = cc * 32 + tok
                            mm(ps[4][:, col:col + 1], sqb[s][:, ti * 128:(ti + 1) * 128], ones, True, True,
                               [r_sqb[s], r_ones], [psr[4]])
        if dbg and l == 0:
            out_ops.append(dma("sp", "dbgy3", dbg_t["recT"], recT.rearrange("p c t -> p (c t)"), r_recT, []))
        if dbg == "C":
            break

        fD = phase_begin(["K", "V", "X1"])
        o = O_K
        h2T = reg(o, 16392, BF16).rearrange("p (k t) -> p k t", k=8); o += 16392
        wo = reg(o, 4096, BF16).rearrange("p (k c) -> p k c", k=8); o += 4096
        xt = [reg(o + 1024 * i, 1024) for i in range(4)]; o += 4096
        ht = [reg(o + 512 * i, 512, BF16) for i in range(3)]; o += 1536
        junk = reg(o, 512, BF16); o += 512
        tmpP = [reg(o + 1024 * i, 1024) for i in range(2)]; o += 2048
        assert o <= O_AT
        LD = ["K", "V", "X1"]
        r_h2T = [newres("h2T%d" % i, fD, LD) for i in range(NT)]
        r_h2pad = newres("h2pad", fD, LD)
        r_wo = newres("wo", fD, LD)
        r_xt = [newres("xtD%d" % i, fD, LD) for i in range(4)]
        r_ht = [newres("htD%d" % i, fD, LD) for i in range(3)]
        r_junk = newres("junkD", fD, LD)
        r_tmpP = [newres("tmpPD%d" % i, fD, LD) for i in range(2)]
        pool("memset", [], [r_h2pad], ap=h2T[:, :, 0:1], constant=0.0)
        pool("memset", [], [r_h2pad], ap=h2T[:, :, 4097:4098], constant=0.0)
        dma("pool", "wo", wo, w_out_d[l].rearrange("(k p) c -> p k c", p=128), [], [r_wo])
        dma("sp", "gA", gA, g_ffn_d[l:l + 1, :].partition_broadcast(128), [], [r_gA])
        tmpa = stat[:, 16:48]
        r_tmpa = Res("tmpa")
        rstd_from_ss(ss_att, rstd_att, D_ATT, r_ssatt, r_rstda, tmpa, r_tmpa)
        ssr = stat[:, 48:80] if False else reg(936, 32)
        r_ssr = Res("ssr")
        dve("tensor_copy", [psr[4]], [r_ssr], out=ssr, in_=ps[4][:, 0:32])
        for cc_ in range(1, 4):
            dve("tensor_tensor", [psr[4], r_ssr], [r_ssr], out=ssr, in0=ssr, in1=ps[4][:, cc_ * 32:cc_ * 32 + 32], op=ALU.add)
        rstd_from_ss(ssr, rstd_rec, D_LRU, r_ssr, r_rstdr, tmpa, r_tmpa)
        def D_S0(i):
            sl = i % 4
            dma("sp", "xtD%d" % sl, xt[sl], src_tiles[i], [r_xs[i]], [r_xt[sl]])

        def stage1D(i):
            sl, s2 = i % 4, i % 2
            for c in range(2):
                cs = slice(c * 512, (c + 1) * 512)
                for k in range(4):
                    mm(ps[2 * c], attT[:, k, i * 128:(i + 1) * 128], wo[:, k, cs], k == 0, k == 3,
                       [r_attT[i], r_wo], [psr[2 * c]])
                for k in range(4):
                    mm(ps[2 * c + 1], recT[:, k, i * 128:(i + 1) * 128], wo[:, 4 + k, cs], k == 0, k == 3,
                       [r_recT[i // 4], r_wo], [psr[2 * c + 1]])
                dve("scalar_tensor_tensor", [psr[2 * c], r_rstda, r_xt[sl]], [r_xt[sl]], out=xt[sl][:, cs],
                    in0=ps[2 * c], scalar=rstd_att[:, i:i + 1], in1=xt[sl][:, cs], op0=ALU.mult, op1=ALU.add)
                dve("scalar_tensor_tensor", [psr[2 * c + 1], r_rstdr, r_xt[sl]], [r_xt[sl]], out=xt[sl][:, cs],
                    in0=ps[2 * c + 1], scalar=rstd_rec[:, i:i + 1], in1=xt[sl][:, cs], op0=ALU.mult, op1=ALU.add)
            dma("sp", "xsD%d" % sl, xs_tiles[i], xt[sl], [r_xt[sl]], [r_xs[i]])
            ssx, tmpx, rsx = stat[:, 4 * s2:4 * s2 + 1], stat[:, 4 * s2 + 1:4 * s2 + 2], stat[:, 4 * s2 + 2:4 * s2 + 3]
            act(AF.Square, junk, xt[sl], [r_xt[sl]], [r_junk, r_st[s2]], accum_out=ssx)
            rstd_from_ss(ssx, rsx, D, r_st[s2], r_st[s2], tmpx, r_st[s2])

        def D_S2(i):
            sl, hl, s2 = i % 4, i % 3, i % 2
            rsx = stat[:, 4 * s2 + 2:4 * s2 + 3]
            dve("scalar_tensor_tensor", [r_xt[sl], r_st[s2], r_gA], [r_ht[hl]], out=ht[hl], in0=xt[sl],
                scalar=rsx, in1=gA, op0=ALU.mult, op1=ALU.mult)

        def D_S3(i):
            hl, pb = i % 3, 6 + i % 2
            pst = ps[pb].bitcast(BF16)
            for k in range(8):
                tr(pst[:, k * 128:(k + 1) * 128], ht[hl][:, k * 128:(k + 1) * 128], [r_ht[hl]], [psr[pb]])

        def D_S4(i):
            pb = 6 + i % 2
            pst = ps[pb].bitcast(BF16)
            o_ = h2T[:, :, 1 + i * 128:1 + (i + 1) * 128]
            s_ = pst.rearrange("p (k t) -> p k t", k=8)
            act(AF.Copy, o_, s_, [psr[pb]], [r_h2T[i]])

        for i_ in range(3):
            D_S0(i_)
        for t_ in range(NT + 3):
            if t_ < NT:
                stage1D(t_)
            if 0 <= t_ - 1 < NT:
                D_S2(t_ - 1)
            if t_ + 3 < NT:
                D_S0(t_ + 3)
            if 0 <= t_ - 2 < NT:
                D_S3(t_ - 2)
            if 0 <= t_ - 3 < NT:
                D_S4(t_ - 3)
        if dbg and l == 0:
            out_ops.append(dma("sp", "dbgy4", dbg_t["xD"], xs_d, r_xs, []))
        if dbg == "D":
            break

        fE = phase_begin(["X1", "AT", "Q"])
        LE = ["X1", "AT", "Q"]
        xacc = reg(O_X1, 8192).rearrange("p (t d) -> p t d", t=8)
        wbase = O_X1 + 8192
        wua = [reg(wbase + 6144 * i, 2048, BF16).rearrange("p (k c) -> p k c", k=8) for i in range(2)]
        wul = [reg(wbase + 6144 * i + 2048, 2048, BF16).rearrange("p (k c) -> p k c", k=8) for i in range(2)]
        wd = [reg(wbase + 6144 * i + 4096, 2048, BF16).rearrange("p (f c) -> p f c", f=4) for i in range(2)]
        abase = wbase + 12288
        actT = [reg(abase + 2048 * i, 2048, BF16).rearrange("p (f t) -> p f t", f=4) for i in range(2)]
        assert abase + 4096 <= ARENA_KIB * KW
        cbuf = [reg(O_Q + 512 * i, 512) for i in range(2)]
        gbuf = [reg(O_Q + 1024 + 512 * i, 512) for i in range(2)]
        stgE = [reg(O_Q + 2048 + 512 * i, 512) for i in range(2)]
        r_xacc = [newres("xacc%d" % i, fE, LE) for i in range(8)]
        r_wu = [newres("wu%d" % i, fE, LE) for i in range(2)]
        r_wl = [newres("wl%d" % i, fE, LE) for i in range(2)]
        r_wd = [newres("wd%d" % i, fE, LE) for i in range(2)]
        r_actT = [newres("actT%d" % i, fE, LE) for i in range(2)]
        r_cbuf = [newres("cbuf%d" % i, fE, LE) for i in range(2)]
        r_gbuf = [newres("gbuf%d" % i, fE, LE) for i in range(2)]
        r_stgE = [newres("stgE%d" % i, fE, LE) for i in range(2)]
        if last:
            dma("sp", "gA", gA, g_fin_d.partition_broadcast(128), [], [r_gA])
        wup = w_up_d[l].rearrange("(k p) c -> p k c", p=128)
        wdn = w_down_d[l].rearrange("(f p) c -> p f c", p=128)
        groups = [(tcn, G) for tcn in range(4) for G in range(6)]
        if dbg == "E1":
            groups = groups[:1]
        if dbg == "E2":
            groups = groups[:6]
        if dbg == "E3":
            groups = groups[:3]
        ucnt = [0]
        SUBO = [0, 342, 684, 1024]
        pycnt = [0]

        def emit_wload(gi):
            tcn, G = groups[gi]
            ws = gi % 2
            dma("pool", "wua%d" % ws, wua[ws], wup[:, :, G * 512:(G + 1) * 512], [], [r_wu[ws]])
            dma("pool", "wul%d" % ws, wul[ws], wup[:, :, D_FF + G * 512:D_FF + (G + 1) * 512], [], [r_wl[ws]])
            dma("pool", "wd%d" % ws, wd[ws], wdn[:, G * 4:(G + 1) * 4, :], [], [r_wd[ws]])

        def emit_up(gi, only=None):
            tcn, G = groups[gi]
            ws = gi % 2
            for f4 in range(4):
                fch = G * 4 + f4
                w0 = pfl[:, 48 + fch * 3:49 + fch * 3]
                w1 = pfl[:, 49 + fch * 3:50 + fch * 3]
                w2 = pfl[:, 50 + fch * 3:51 + fch * 3]
                bb = pfl[:, 120 + fch:121 + fch]
                for sub in range(3):
                    if only is not None and (f4, sub) != only:
                        continue
                    o0, o1 = SUBO[sub], SUBO[sub + 1]
                    sz = o1 - o0
                    s0 = tcn * 1024 + o0
                    us = ucnt[0] % 2
                    ucnt[0] += 1
                    ua, ul = ps[2 * us][:, 0:sz + 2], ps[2 * us + 1][:, 0:sz]
                    tlo, thi = max((s0 - 1) // 128, 0), min((s0 + sz) // 128, NT - 1)
                    rh = [r_h2T[q] for q in range(tlo, thi + 1)] + [r_h2pad]
                    for k in range(8):
                        mm(ua, wua[ws][:, k, f4 * 128:(f4 + 1) * 128], h2T[:, k, s0:s0 + sz + 2], k == 0, k == 7,
                           rh + [r_wu[ws]], [psr[2 * us]])
                    for k in range(8):
                        mm(ul, wul[ws][:, k, f4 * 128:(f4 + 1) * 128], h2T[:, k, s0 + 1:s0 + 1 + sz], k == 0, k == 7,
                           rh + [r_wl[ws]], [psr[2 * us + 1]])
                    cb, gb = cbuf[us][:, 0:sz], gbuf[us][:, 0:sz]
                    act(AF.Identity, cb, ua[:, 1:sz + 1], [psr[2 * us], r_pf], [r_cbuf[us]], scale=w1, bias=bb)
                    dve("scalar_tensor_tensor", [psr[2 * us], r_pf, r_cbuf[us]], [r_cbuf[us]], out=cb,
                        in0=ua[:, 0:sz], scalar=w0, in1=cb, op0=ALU.mult, op1=ALU.add)
                    dve("scalar_tensor_tensor", [psr[2 * us], r_pf, r_cbuf[us]], [r_cbuf[us]], out=cb,
                        in0=ua[:, 2:sz + 2], scalar=w2, in1=cb, op0=ALU.mult, op1=ALU.add)
                    act(AF.Gelu_apprx_tanh, gb, cb, [r_cbuf[us]], [r_gbuf[us]])
                    dve("tensor_tensor", [r_gbuf[us], psr[2 * us + 1]], [r_actT[ws]],
                        out=actT[ws][:, f4, o0:o1], in0=gb, in1=ul, op=ALU.mult)

        def emit_down(gi, only=None):
            tcn, G = groups[gi]
            ws = gi % 2
            for t in range(8):
                for c in range(2):
                    if only is not None and (t, c) != only:
                        continue
                    pb = 5 + pycnt[0] % 2
                    pycnt[0] += 1
                    cs = slice(c * 512, (c + 1) * 512)
                    for f4 in range(4):
                        mm(ps[pb], actT[ws][:, f4, t * 128:(t + 1) * 128], wd[ws][:, f4, cs], f4 == 0, f4 == 3,
                           [r_actT[ws], r_wd[ws]], [psr[pb]])
                    dve("tensor_tensor", [psr[pb], r_xacc[t]], [r_xacc[t]], out=xacc[:, t, cs], in0=ps[pb],
                        in1=xacc[:, t, cs], op=ALU.add)

        def emit_xload_tile(tcn, t):
            i = 8 * tcn + t
            dma("sp", "xacc%d" % t, xacc[:, t, :], xs_tiles[i], [r_xs[i]], [r_xacc[t]])

        def emit_xload(tcn):
            for t in range(8):
                emit_xload_tile(tcn, t)

        def emit_xstore(tcn, nxt):
            for t in range(8):
                i = 8 * tcn + t
                if not last:
                    dma("sp", "xst%d" % t, xs_tiles[i], xacc[:, t, :], [r_xacc[t]], [r_xs[i]])
                else:
                    sl = t % 2
                    ssx, tmpx, rsx = (stat[:, 4 * sl:4 * sl + 1], stat[:, 4 * sl + 1:4 * sl + 2],
                                      stat[:, 4 * sl + 2:4 * sl + 3])
                    act(AF.Square, cbuf[sl].bitcast(BF16), xacc[:, t, :], [r_xacc[t]], [r_cbuf[sl], r_st[sl]],
                        accum_out=ssx)
                    rstd_from_ss(ssx, rsx, D, r_st[sl], r_st[sl], tmpx, r_st[sl])
                    dve("scalar_tensor_tensor", [r_xacc[t], r_st[sl], r_gA], [r_xacc[t]], out=xacc[:, t, :],
                        in0=xacc[:, t, :], scalar=rsx, in1=gA, op0=ALU.mult, op1=ALU.mult)
                    out_ops.append(dma("sp", "yst%d" % t, y_tiles[i], xacc[:, t, :], [r_xacc[t]], []))
                if nxt:
                    emit_xload_tile(tcn + 1, t)

        emit_wload(0)
        emit_xload(0)
        emit_up(0)
        for gi in range(len(groups)):
            tcn, G = groups[gi]
            downs = [(t, c) for t in range(8) for c in range(2)]
            if gi + 1 < len(groups):
                emit_wload(gi + 1)
                ups = [(f4, sub) for f4 in range(4) for sub in range(3)]
                nd = 0
                for i_, u_ in enumerate(ups):
                    emit_up(gi + 1, only=u_)
                    tgt = (len(downs) * (i_ + 1)) // len(ups)
                    while nd < tgt:
                        emit_down(gi, only=downs[nd])
                        nd += 1
            else:
                for d_ in downs:
                    emit_down(gi, only=d_)
            if G == 5:
                emit_xstore(tcn, tcn < 3 and gi + 1 < len(groups))

    P.emit("sp", lambda e: e.nop(), deps=out_ops)
    P.finalize()
    lanes = list(P.lanes.keys())
    semctx = [nc.semaphore("s_" + ln) for ln in lanes]
    sems = {ln: c.__enter__() for ln, c in zip(lanes, semctx)}
    with nc.Block() as block:
        @block.sync
        def _(e):
            P.replay("sp", e, sems)

        @block.gpsimd
        def _(e):
            P.replay("pool", e, sems)

        @block.scalar
        def _(e):
            P.replay("act", e, sems)

        @block.vector
        def _(e):
            P.replay("dve", e, sems)

        @block.tensor
        def _(e):
            P.replay("pe", e, sems)
    for c in reversed(semctx):
        c.__exit__(None, None, None)
    psctx.__exit__(None, None, None)
    ctx.__exit__(None, None, None)
    nops = {k: len(v) for k, v in P.ops.items()}
    return nc, nops


_CACHE = {}


def kernel(**inputs):
    depth = DEPTH
    shared = _prep_shared(inputs, depth)
    shared["idn"] = np.eye(128, dtype=np.float32)
    x = np.asarray(inputs["x"], np.float32)
    nb = x.shape[0]
    if "nc" not in _CACHE:
        _CACHE["nc"] = build(depth)[0]
    nc = _CACHE["nc"]
    in_maps = []
    for b in range(nb):
        m = dict(shared)
        m["x"] = np.ascontiguousarray(x[b])
        in_maps.append(m)
    res = run_bass_kernel_spmd(nc, in_maps, core_ids=list(range(nb)))
    return np.stack([np.asarray(r["y"], np.float32) for r in res.results], axis=0)
```
